# Optimizing a Trainium2 kernel written in Bass

```python
import math
import jax, jax.numpy as jnp
from jax import lax
import numpy as np

D_MODEL = 1024
BATCH = 16
SEQ = 256
DEPTH = 2
DEC_BATCH = 2
DEC_SEQ = 4096
PAST_LEN = 256

GRID_W = 64
N_MIXERS = 2
N_HEADS = 16
N_KV_HEADS = 4
HEAD_DIM = 64
GROUP = N_HEADS // N_KV_HEADS
QKV_DIM = (N_HEADS + 2 * N_KV_HEADS) * HEAD_DIM
WINDOW = 128
BLOCK = 128
ROPE_BASE = 10000.0
N_FREQ = HEAD_DIM // 4
POOL_WINDOWS = (2, 4, 8, 16)
N_POOL_GROUPS = 4
POOL_GROUP_DIM = D_MODEL // N_POOL_GROUPS
D_FF = 2816
N_ATTN_LAYERS = (DEPTH + 1) // 2
N_POOL_LAYERS = DEPTH // 2
N_MOD = 9
LN_EPS = 1e-5
DEEPNORM_ALPHA = (2.0 * DEPTH) ** 0.25
DEEPNORM_BETA = (8.0 * DEPTH) ** -0.25
ATTN_SCALE = HEAD_DIM ** -0.5
NEG_INF = -1e30

kernel_name = "hybrid_diffusion_window_gqa_pool_macaron_step"


def layer_norm(x, g, b):
    xf = x.astype(jnp.float32)
    mu = xf.mean(-1, keepdims=True)
    var = jnp.square(xf - mu).mean(-1, keepdims=True)
    y = (xf - mu) * lax.rsqrt(var + LN_EPS)
    return (y * g.astype(jnp.float32) + b.astype(jnp.float32)).astype(x.dtype)


def residual_post_norm(x, out, gate, g, b):
    return layer_norm(DEEPNORM_ALPHA * x + gate * out, g, b)


def adaln_params(cond, w_mod, b_mod):
    m = (jax.nn.silu(cond) @ w_mod + b_mod)[..., None, :]
    return jnp.split(m, N_MOD, axis=-1)


def modulate(x, shift, scale):
    return x * (1.0 + scale) + shift


def swiglu(x, w_gate, w_up, w_down):
    return (jax.nn.silu(x @ w_gate) * (x @ w_up)) @ w_down


def axial_rope_tables(n_rows, dtype):
    rows = jnp.repeat(jnp.arange(n_rows, dtype=jnp.float32), GRID_W)
    cols = jnp.tile(jnp.arange(GRID_W, dtype=jnp.float32), n_rows)
    inv = jnp.power(ROPE_BASE, -jnp.arange(N_FREQ, dtype=jnp.float32) / N_FREQ)
    ang_r = rows[:, None] * inv
    ang_c = cols[:, None] * inv
    ang = jnp.concatenate([ang_r, ang_r, ang_c, ang_c], axis=-1)
    return jnp.cos(ang).astype(dtype), jnp.sin(ang).astype(dtype)


def apply_axial_rope(x, cos, sin):
    r1, r2, c1, c2 = jnp.split(x, 4, axis=-1)
    rot = jnp.concatenate([-r2, r1, -c2, c1], axis=-1)
    return x * cos[None, :, None, :] + rot * sin[None, :, None, :]


def split_qkv(h, w_qkv):
    B, S, _ = h.shape
    qkv = h @ w_qkv
    q, k, v = jnp.split(qkv, [N_HEADS * HEAD_DIM, (N_HEADS + N_KV_HEADS) * HEAD_DIM], axis=-1)
    q = q.reshape(B, S, N_HEADS, HEAD_DIM)
    k = k.reshape(B, S, N_KV_HEADS, HEAD_DIM)
    v = v.reshape(B, S, N_KV_HEADS, HEAD_DIM)
    return q, k, v


def to_query_blocks(q):
    B, S = q.shape[:2]
    qb = q.reshape(B, S // BLOCK, BLOCK, N_KV_HEADS, GROUP, HEAD_DIM)
    return qb.transpose(1, 0, 2, 3, 4, 5)


def from_query_blocks(o):
    nb, B = o.shape[:2]
    return o.transpose(1, 0, 2, 3, 4, 5).reshape(B, nb * BLOCK, N_HEADS * HEAD_DIM)


def sink_column(sink, score_shape):
    s = sink.astype(jnp.float32).reshape(N_KV_HEADS, GROUP)[None, :, :, None, None]
    return jnp.broadcast_to(s, score_shape[:-1] + (1,))


def context_attention(q, k, v, sink):
    L = k.shape[1]

    def one_block(q_blk):
        s = jnp.einsum('bqkgd,bjkd->bkgqj', q_blk, k).astype(jnp.float32) * ATTN_SCALE
        logits = jnp.concatenate([s, sink_column(sink, s.shape)], axis=-1)
        p = jax.nn.softmax(logits, axis=-1)[..., :L].astype(v.dtype)
        return jnp.einsum('bkgqj,bjkd->bqkgd', p, v)

    return from_query_blocks(lax.map(one_block, to_query_blocks(q)))


def latent_attention(q, k, v, k_ctx, v_ctx, sink):
    S = q.shape[1]
    L = k_ctx.shape[1]
    nb = S // BLOCK
    pad = ((0, 0), (BLOCK, BLOCK), (0, 0), (0, 0))
    k_pad = jnp.pad(k, pad)
    v_pad = jnp.pad(v, pad)
    offs_q = jnp.arange(BLOCK)
    offs_k = jnp.arange(3 * BLOCK) - BLOCK

    def one_block(args):
        q_blk, b = args
        start = b * BLOCK
        kb = lax.dynamic_slice_in_dim(k_pad, start, 3 * BLOCK, axis=1)
        vb = lax.dynamic_slice_in_dim(v_pad, start, 3 * BLOCK, axis=1)
        qpos = start + offs_q
        kpos = start + offs_k
        valid = ((jnp.abs(qpos[:, None] - kpos[None, :]) <= WINDOW)
                 & (kpos >= 0)[None, :] & (kpos < S)[None, :])
        s_loc = jnp.einsum('bqkgd,bjkd->bkgqj', q_blk, kb).astype(jnp.float32) * ATTN_SCALE
        s_loc = jnp.where(valid, s_loc, NEG_INF)
        s_ctx = jnp.einsum('bqkgd,bjkd->bkgqj', q_blk, k_ctx).astype(jnp.float32) * ATTN_SCALE
        logits = jnp.concatenate([s_loc, s_ctx, sink_column(sink, s_loc.shape)], axis=-1)
        p = jax.nn.softmax(logits, axis=-1).astype(v.dtype)
        p_loc = p[..., :3 * BLOCK]
        p_ctx = p[..., 3 * BLOCK:3 * BLOCK + L]
        return (jnp.einsum('bkgqj,bjkd->bqkgd', p_loc, vb)
                + jnp.einsum('bkgqj,bjkd->bqkgd', p_ctx, v_ctx))

    return from_query_blocks(lax.map(one_block, (to_query_blocks(q), jnp.arange(nb))))


def attn_context(h, w_qkv, w_o, sink):
    q, k, v = split_qkv(h, w_qkv)
    o = context_attention(q, k, v, sink)
    return o @ w_o, k, v


def attn_latent(h, k_ctx, v_ctx, cos, sin, w_qkv, w_o, sink):
    q, k, v = split_qkv(h, w_qkv)
    q = apply_axial_rope(q, cos, sin)
    k = apply_axial_rope(k, cos, sin)
    o = latent_attention(q, k, v, k_ctx, v_ctx, sink)
    return o @ w_o


def multiscale_pool(h, w_pool, scale):
    B, S, D = h.shape
    hf = h.astype(jnp.float32)
    cs = jnp.concatenate([jnp.zeros((B, 1, D), jnp.float32), jnp.cumsum(hf, axis=1)], axis=1)
    t = jnp.arange(S)
    outs = []
    for gi, w in enumerate(POOL_WINDOWS):
        lo = jnp.clip(t - w // 2, 0, S)
        hi = jnp.clip(t + w // 2, 0, S)
        sl = slice(gi * POOL_GROUP_DIM, (gi + 1) * POOL_GROUP_DIM)
        cg = cs[..., sl]
        total = jnp.take(cg, hi, axis=1) - jnp.take(cg, lo, axis=1)
        cnt = (hi - lo).astype(jnp.float32)[None, :, None]
        pooled = (total / cnt - hf[..., sl]).astype(h.dtype)
        outs.append(pooled @ w_pool[gi])
    return jnp.concatenate(outs, axis=-1) * scale


def setup_inputs(seed: int = 0) -> dict:
    key = jax.random.key(seed)
    ks = jax.random.split(key, 20)
    f32 = jnp.float32
    nrm = lambda k, shape: jax.random.normal(k, shape, f32)
    d_inner = N_HEADS * HEAD_DIM
    return {
        'x_prompt': nrm(ks[0], (BATCH, SEQ, D_MODEL)),
        'x_sample': nrm(ks[1], (DEC_BATCH, DEC_SEQ, D_MODEL)),
        'cache_k': nrm(ks[2], (DEC_BATCH, N_ATTN_LAYERS, PAST_LEN, N_KV_HEADS, HEAD_DIM)),
        'cache_v': nrm(ks[3], (DEC_BATCH, N_ATTN_LAYERS, PAST_LEN, N_KV_HEADS, HEAD_DIM)),
        'c': nrm(ks[4], (DEC_BATCH, D_MODEL)),
        'c_ctx': nrm(ks[5], (D_MODEL,)),
        'w_mod': nrm(ks[6], (DEPTH, D_MODEL, N_MOD * D_MODEL)) * (0.5 * D_MODEL ** -0.5),
        'b_mod': nrm(ks[7], (DEPTH, N_MOD * D_MODEL)) * 0.01,
        'ln_g': 1.0 + 0.05 * nrm(ks[8], (DEPTH, 3, D_MODEL)),
        'ln_b': 0.02 * nrm(ks[9], (DEPTH, 3, D_MODEL)),
        'ffn_w_gate': nrm(ks[10], (DEPTH, 2, D_MODEL, D_FF)) * D_MODEL ** -0.5,
        'ffn_w_up': nrm(ks[11], (DEPTH, 2, D_MODEL, D_FF)) * D_MODEL ** -0.5,
        'ffn_w_down': nrm(ks[12], (DEPTH, 2, D_FF, D_MODEL)) * (DEEPNORM_BETA * D_FF ** -0.5),
        'attn_w_qkv': nrm(ks[13], (N_ATTN_LAYERS, D_MODEL, QKV_DIM)) * D_MODEL ** -0.5,
        'attn_w_o': nrm(ks[14], (N_ATTN_LAYERS, d_inner, D_MODEL)) * (DEEPNORM_BETA * d_inner ** -0.5),
        'attn_sink': 0.5 * nrm(ks[15], (N_ATTN_LAYERS, N_HEADS)),
        'pool_w': nrm(ks[16], (N_POOL_LAYERS, N_POOL_GROUPS, POOL_GROUP_DIM, POOL_GROUP_DIM)) * (DEEPNORM_BETA * POOL_GROUP_DIM ** -0.5),
        'pool_scale': 1.0 + 0.1 * nrm(ks[17], (N_POOL_LAYERS, D_MODEL)),
    }


def reference(x_prompt, x_sample, cache_k, cache_v, c, c_ctx, w_mod, b_mod, ln_g, ln_b,
              ffn_w_gate, ffn_w_up, ffn_w_down, attn_w_qkv, attn_w_o, attn_sink,
              pool_w, pool_scale):
    n_rows = x_sample.shape[1] // GRID_W
    cos, sin = axial_rope_tables(n_rows, x_sample.dtype)
    xp, xs = x_prompt, x_sample
    new_k, new_v = [], []
    for i in range(DEPTH):
        mix = i % N_MIXERS
        j = i // N_MIXERS
        mp = adaln_params(c_ctx, w_mod[i], b_mod[i])
        ms = adaln_params(c, w_mod[i], b_mod[i])

        fp = swiglu(modulate(xp, mp[0], mp[1]), ffn_w_gate[i, 0], ffn_w_up[i, 0], ffn_w_down[i, 0])
        fs = swiglu(modulate(xs, ms[0], ms[1]), ffn_w_gate[i, 0], ffn_w_up[i, 0], ffn_w_down[i, 0])
        xp = residual_post_norm(xp, 0.5 * fp, mp[2], ln_g[i, 0], ln_b[i, 0])
        xs = residual_post_norm(xs, 0.5 * fs, ms[2], ln_g[i, 0], ln_b[i, 0])

        hp = modulate(xp, mp[3], mp[4])
        hs = modulate(xs, ms[3], ms[4])
        if mix == 0:
            op, k_p, v_p = attn_context(hp, attn_w_qkv[j], attn_w_o[j], attn_sink[j])
            new_k.append(k_p)
            new_v.append(v_p)
            os_ = attn_latent(hs, cache_k[:, j], cache_v[:, j], cos, sin,
                              attn_w_qkv[j], attn_w_o[j], attn_sink[j])
        else:
            op = multiscale_pool(hp, pool_w[j], pool_scale[j])
            os_ = multiscale_pool(hs, pool_w[j], pool_scale[j])
        xp = residual_post_norm(xp, op, mp[5], ln_g[i, 1], ln_b[i, 1])
        xs = residual_post_norm(xs, os_, ms[5], ln_g[i, 1], ln_b[i, 1])

        fp = swiglu(modulate(xp, mp[6], mp[7]), ffn_w_gate[i, 1], ffn_w_up[i, 1], ffn_w_down[i, 1])
        fs = swiglu(modulate(xs, ms[6], ms[7]), ffn_w_gate[i, 1], ffn_w_up[i, 1], ffn_w_down[i, 1])
        xp = residual_post_norm(xp, 0.5 * fp, mp[8], ln_g[i, 2], ln_b[i, 2])
        xs = residual_post_norm(xs, 0.5 * fs, ms[8], ln_g[i, 2], ln_b[i, 2])

    state_k = jnp.stack(new_k, axis=1)
    state_v = jnp.stack(new_v, axis=1)
    return (xp, xs, state_k, state_v)
```

```python
import math
from contextlib import ExitStack
import numpy as np
import concourse.bass as bass
import concourse.mybir as mybir
from concourse.bass_utils import run_bass_kernel_spmd

F32 = mybir.dt.float32
BF16 = mybir.dt.bfloat16
AF = mybir.ActivationFunctionType
ALU = mybir.AluOpType

ENGS = ("pe", "act", "dve", "pool", "sp")


class _Rec:
    def __getattr__(self, name):
        def f(*args, **kwargs):
            return (name, args, kwargs)
        return f


I = _Rec()


SAME_ENG_DIST = 10 ** 9
ATT_CUT = 99
LN_XNEW_ENG = "pool"


class Op:
    __slots__ = ("eng", "fn", "deps", "idx", "dma", "stream", "signal", "cum")

    def __init__(self, eng, fn, idx, dma, stream):
        self.eng = eng
        self.fn = fn
        self.deps = set()
        self.idx = idx
        self.dma = dma
        self.stream = stream
        self.signal = False
        self.cum = 0


class Prog:
    def __init__(self, nc, same_eng_dist=None):
        self.nc = nc
        self.eng_ops = {e: [] for e in ENGS}
        self.last_writer = {}
        self.readers = {}
        self.streams = {}
        self.same_eng_dist = SAME_ENG_DIST if same_eng_dist is None else same_eng_dist
        self.final_ops = []

    def op(self, eng, fn, reads=(), writes=(), stream=None, final=False):
        dma = stream is not None
        lst = self.eng_ops[eng]
        o = Op(eng, fn, len(lst), dma, stream)
        lst.append(o)
        for k in reads:
            w = self.last_writer.get(k)
            if w is not None:
                o.deps.add(w)
            self.readers.setdefault(k, []).append(o)
        for k in writes:
            w = self.last_writer.get(k)
            if w is not None:
                o.deps.add(w)
            for r in self.readers.get(k, ()):
                if r is not o:
                    o.deps.add(r)
            self.last_writer[k] = o
            self.readers[k] = []
        if dma:
            self.streams.setdefault(stream, []).append(o)
        if final:
            self.final_ops.append(o)
        return o

    def _prune(self, o):
        best = {}
        for d in o.deps:
            if d is o:
                continue
            if (not d.dma) and d.eng == o.eng:
                if o.eng == "pe":
                    continue
                if o.idx - d.idx >= self.same_eng_dist:
                    continue
            key = ("S_" + d.stream) if d.dma else ("E_" + d.eng)
            b = best.get(key)
            if b is None or d.idx > b.idx:
                best[key] = d
        return best

    def emit(self, block, sems):
        pruned = {}
        for e in ENGS:
            for o in self.eng_ops[e]:
                ds = self._prune(o)
                pruned[id(o)] = ds
                for d in ds.values():
                    if not d.dma:
                        d.signal = True
        for e in ENGS:
            c = 0
            for o in self.eng_ops[e]:
                if (not o.dma) and o.signal:
                    c += 1
                    o.cum = c
        for s, lst in self.streams.items():
            c = 0
            for o in lst:
                c += 16
                o.cum = c

        def emit_engine(eng_name, eng):
            waited = {}
            for o in self.eng_ops[eng_name]:
                for key, d in pruned[id(o)].items():
                    v = d.cum
                    if waited.get(key, 0) >= v:
                        continue
                    eng.wait_ge(sems[key], v)
                    waited[key] = v
                name, args, kwargs = o.fn
                ins = getattr(eng, name)(*args, **kwargs)
                if o.dma:
                    ins.then_inc(sems["S_" + o.stream], 16)
                elif o.signal:
                    ins.then_inc(sems["E_" + o.eng], 1)
            if eng_name == "sp":
                for o in self.final_ops:
                    key = "S_" + o.stream
                    v = self.streams[o.stream][-1].cum
                    if waited.get(key, 0) < v:
                        eng.wait_ge(sems[key], v)
                        waited[key] = v

        @block.tensor
        def _(eng):
            emit_engine("pe", eng)

        @block.scalar
        def _(eng):
            emit_engine("act", eng)

        @block.vector
        def _(eng):
            emit_engine("dve", eng)

        @block.gpsimd
        def _(eng):
            emit_engine("pool", eng)

        @block.sync
        def _(eng):
            emit_engine("sp", eng)

    def sem_names(self):
        return ["E_" + e for e in ENGS if e != "sp"] + ["S_" + s for s in self.streams]


D = 1024
KC = 8
DFF = 2816
FC = 22
NT = 1808
SF0 = 512
SFN = 1296
M0 = 640
MN = 1040
OWN0 = 648
OWNN = 1024
HALO = 136
ALPHA = 4.0 ** 0.25
EPS2 = 1e-5 / (ALPHA * ALPHA)
ATT_SCALE = 0.125
POOL_W = (2, 4, 8, 16)
SEG_BOUNDS = [0, 512, 640, 944, 992, 1344, 1376, 1680, 1808]
FULL_BLOCKS = [(0, 512, 0), (512, 432, 1), (944, 432, 1), (1376, 432, 1)]
MAIN_BLOCKS = [(0, 512, 0), (640, 352, 1), (992, 352, 1), (1344, 336, 1)]
HPW = 1584
HP_SEGS = [(8, 0, 256), (272, 256, 256), (536, 640, 1040)]
RING_ELEMS = 2816
NRING = 3


def hpcol(a):
    for (hp, xa, nn) in HP_SEGS:
        if xa <= a < xa + nn:
            return hp + (a - xa)
    raise ValueError(a)


def segs(a, b):
    out = []
    for i in range(len(SEG_BOUNDS) - 1):
        if SEG_BOUNDS[i] < b and SEG_BOUNDS[i + 1] > a:
            out.append(i)
    return out


def xk(c, a, b):
    return [f"x{c}s{s}" for s in segs(a, b)]


def hk(c, a, b):
    return [f"h{c}s{s}" for s in segs(a, b)]


def build_program(upto=99):
    nc = bass.Bass("TRN2", target_bir_lowering=False)
    dt_in = lambda name, shape: nc.dram_tensor(name, shape, F32, kind="ExternalInput").ap()
    d_xT = dt_in("xT", [128, KC, NT])
    d_cond = dt_in("condT", [128, KC, 2])
    d_wmod = dt_in("wmod", [2, 72, 128, KC * 128])
    d_bmod = dt_in("bmodT", [128, 2, 72])
    d_lng = dt_in("lngT", [128, 2, 3, KC])
    d_lnb = dt_in("lnbT", [128, 2, 3, KC])
    d_wgu = dt_in("wgu", [2, 2, FC, 128, 2 * KC * 128])
    d_wd = dt_in("wd", [2, 2, KC, 128, FC * 128])
    d_wqk = dt_in("wqk", [24, 128, KC * 128])
    d_wv = dt_in("wv", [128, KC * 256])
    d_wo = dt_in("wo", [KC, 128, KC * 128])
    d_poolw = dt_in("poolw", [KC, 128, 2 * 128])
    d_pscale = dt_in("pscaleT", [128, KC])
    d_sink = dt_in("sinkb", [128, 16])
    d_cos = dt_in("ropecos", [128, SFN])
    d_sin = dt_in("ropesin", [128, SFN])
    d_ck = dt_in("ckT", [128, 4 * 256])
    d_cv = dt_in("cv", [128, 2, 4, 64])
    d_masks = dt_in("masks", [128, 2 * 128])
    d_kvalid = dt_in("kvalid", [128, 11])
    d_pinv = dt_in("pinv", [128, 4 * 6 * 8])
    d_pvalid = dt_in("pvalid", [128, 16])
    d_y = nc.dram_tensor("yT", [128, KC, 1536], F32, kind="ExternalOutput").ap()
    d_sk = nc.dram_tensor("skT", [64, 4, 512], F32, kind="ExternalOutput").ap()
    d_sv = nc.dram_tensor("sv", [128, 4, 256], F32, kind="ExternalOutput").ap()

    sb = lambda name, shape, dt: nc.alloc_sbuf_tensor(name, shape, dt).ap()
    xT = sb("xT_sb", [128, KC, NT], F32)
    hT = sb("hT_sb", [128, KC, NT], BF16)
    BIG = sb("big", [128, FC * NT], BF16)
    actT = BIG.rearrange("p (f n) -> p f n", f=FC)
    ring = [sb(f"ring{i}", [128, RING_ELEMS], BF16) for i in range(NRING)]
    sg = [sb(f"sg{i}", [128, 512], F32) for i in range(2)]
    zb = [sb(f"zb{i}", [128, 512], BF16) for i in range(2)]
    sqb = [sb(f"sqb{i}", [128, 512], BF16) for i in range(2)]
    lnm = sb("lnm", [128, 512], F32)
    lnv = sb("lnv", [128, 512], F32)
    pT = [sb(f"pT{i}", [128, 512], BF16) for i in range(3)]
    condT = sb("cond_sb", [128, KC, 2], F32)
    scT = sb("scT", [128, KC, 2], BF16)
    modT = sb("modT", [128, 2, 72, 2], F32)
    bmodT = sb("bmod_sb", [128, 2, 72], F32)
    lng = sb("lng_sb", [128, 2, 3, KC], F32)
    lnb = sb("lnb_sb", [128, 2, 3, KC], F32)
    pscale = sb("pscale_sb", [128, KC], F32)
    czT = sb("czT", [128, 6, KC, 2], F32)
    GpT = sb("GpT", [128, 6, KC, 2], F32)
    BpT = sb("BpT", [128, 6, KC, 2], F32)
    tmpc = sb("tmpc", [128, KC, 2], F32)
    S0T = sb("S0T", [128, KC, 2], F32)
    onesm = sb("onesm", [128, 128], BF16)
    onesf = sb("onesf", [128, 128], F32)
    epsb = sb("epsb", [128, 1], F32)
    esink = sb("esink", [128, 16], F32)
    masks = sb("masks_sb", [128, 2, 128], BF16)
    kvalid = sb("kvalid_sb", [128, 11], F32)
    pinv = sb("pinv_sb", [128, 4, 6, 8], F32)
    pvalid = sb("pvalid_sb", [128, 16], F32)
    dsb = sb("dsb", [128, 512], F32)
    ar2 = sb("ar2", [128, KC * 128], BF16)
    ar3 = sb("ar3", [128, KC * 128], BF16)
    cvb = sb("cvb", [128, 2, 4, 64], BF16)
    dummy = sb("dummy_sb", [128, 8], F32)
    etmp = {"dve": sb("etmp_dve", [128, 2, 8], F32), "pool": sb("etmp_pool", [128, 2, 8], F32)}

    ps = [nc.alloc_psum_tensor(f"ps{i}", [128, 512], F32).ap() for i in range(8)]

    P = Prog(nc)
    st = {"ring": 0, "mm": 0, "sg": 0, "zb": 0, "pT": 0, "ld": 0, "ada": 0}

    def sp_load(dst, src, key, reads=()):
        st["ld"] += 1
        P.op("sp", I.dma_start(out=dst, in_=src), reads=list(reads), writes=[key], stream=f"ld{st['ld']}")

    def wload(src, nelem):
        s = st["ring"] % NRING
        st["ring"] += 1
        dst = ring[s][:, 0:nelem]
        P.op("pool", I.dma_start(out=dst, in_=src, max_dma_last_dim=4096),
             writes=[f"ring{s}"], stream=f"ring{s}")
        return ring[s], f"ring{s}"

    def fence(key):
        P.op("pool", I.memset(dummy[:, 0:1], 0.0), writes=[key])

    def next_mm():
        b = st["mm"] % 4
        st["mm"] += 1
        return b

    for c in range(KC):
        for (a, n, _) in FULL_BLOCKS[:1] + [(512, 1296, 1)]:
            pass
    for c in range(KC):
        st["ld"] += 1
        P.op("sp", (lambda c: I.dma_start(out=xT[:, c, :], in_=d_xT[:, c, :]))(c),
             writes=xk(c, 0, NT), stream=f"ld{st['ld']}")
        if c == 0:
            sp_load(condT, d_cond, "cond")
            sp_load(bmodT, d_bmod, "bmod")
            sp_load(lng, d_lng, "lng")
            sp_load(lnb, d_lnb, "lnb")
    sp_load(pscale, d_pscale, "pscale")
    sp_load(esink, d_sink, "esink")
    sp_load(kvalid, d_kvalid, "kvalid")
    sp_load(pinv.rearrange("p a b c -> p (a b c)"), d_pinv, "pinv")
    sp_load(pvalid, d_pvalid, "pvalid")
    P.op("pool", I.memset(onesm, 1.0 / 1024.0), writes=["onesm"])
    P.op("pool", I.memset(onesf, 1.0), writes=["onesf"])
    P.op("pool", I.memset(epsb, EPS2), writes=["epsb"])
    P.op("act", I.activation(out=scT, in_=condT, func=AF.Silu), reads=["cond"], writes=["scT"])

    ada_slots = [(dsb.bitcast(BF16), ["dsb0", "dsb1"], "ar0"),
                 (ar2, ["ar2"], "ar2"), (ar3, ["ar3"], "ar3")]
    ada_inflight = []

    def ada_issue(i, m, c):
        slot, key, stream = ada_slots[st["ada"] % len(ada_slots)]
        st["ada"] += 1
        oc = m * 8 + c
        P.op("pool", I.dma_start(out=slot[:, 0:KC * 128], in_=d_wmod[i, oc], max_dma_last_dim=4096),
             writes=list(key), stream=stream)
        ada_inflight.append((i, m, c, slot, key))

    def ada_compute():
        while ada_inflight:
            i, m, c, slot, key = ada_inflight.pop(0)
            col0 = (i * 9 + m) * 16
            w = slot[:, 0:KC * 128].rearrange("p (k n) -> p k n", k=KC)
            for kc in range(KC):
                P.op("pe", I.matmul(ps[7][:, col0 + 2 * c: col0 + 2 * c + 2], w[:, kc, :], scT[:, kc, :],
                                    start=(kc == 0), stop=(kc == KC - 1)),
                     reads=list(key) + ["scT"], writes=["ps7"])
            if c == KC - 1:
                src = ps[7][:, col0:col0 + 16].rearrange("p (c j) -> p c j", c=KC)
                bm = bmodT[:, i, m * 8:(m + 1) * 8].unsqueeze(2).broadcast_to([128, KC, 2])
                P.op("dve", I.tensor_tensor(out=modT[:, i, m * 8:(m + 1) * 8, :], in0=src, in1=bm, op=ALU.add),
                     reads=["ps7", "bmod"], writes=[f"mod{i}_{m}"])

    def emit_adaln(i, m):
        for c0 in range(0, KC, 4):
            for c in range(c0, c0 + 4):
                if len(ada_inflight) >= len(ada_slots):
                    ada_compute()
                ada_issue(i, m, c)
            ada_compute()

    def modv(i, m):
        return modT[:, i, m * 8:(m + 1) * 8, :]

    def emit_coefs(i, j):
        q = i * 3 + j
        gate = modv(i, 3 * j + 2)
        if i == 1 and j == 1:
            psb = pscale.unsqueeze(2).broadcast_to([128, KC, 2])
            P.op("dve", I.scalar_tensor_tensor(out=czT[:, q], in0=gate, scalar=1.0 / ALPHA, in1=psb,
                                                          op0=ALU.mult, op1=ALU.mult),
                 reads=[f"mod{i}_{3 * j + 2}", "pscale"], writes=[f"cz{q}"])
        else:
            fac = (1.0 if j == 1 else 0.5) / ALPHA
            P.op("dve", I.tensor_scalar(out=czT[:, q], in0=gate, scalar1=fac, scalar2=None, op0=ALU.mult),
                 reads=[f"mod{i}_{3 * j + 2}"], writes=[f"cz{q}"])
        if i == 1 and j == 2:
            return
        ni, nj = (i, j + 1) if j < 2 else (i + 1, 0)
        shift, scale = modv(ni, 3 * nj), modv(ni, 3 * nj + 1)
        rk = [f"mod{ni}_{3 * nj}", f"mod{ni}_{3 * nj + 1}", "lng", "lnb"]
        gb = lng[:, i, j, :].unsqueeze(2).broadcast_to([128, KC, 2])
        bb = lnb[:, i, j, :].unsqueeze(2).broadcast_to([128, KC, 2])
        P.op("dve", I.tensor_scalar(out=tmpc, in0=scale, scalar1=1.0, scalar2=None, op0=ALU.add),
             reads=rk, writes=["tmpc"])
        P.op("dve", I.tensor_tensor(out=GpT[:, q], in0=tmpc, in1=gb, op=ALU.mult),
             reads=["tmpc"] + rk, writes=[f"Gp{q}"])
        P.op("dve", I.tensor_tensor(out=BpT[:, q], in0=tmpc, in1=bb, op=ALU.mult),
             reads=["tmpc"] + rk, writes=[f"Bp{q}"])
        P.op("dve", I.tensor_tensor(out=BpT[:, q], in0=BpT[:, q], in1=shift, op=ALU.add),
             reads=[f"Bp{q}"] + rk, writes=[f"Bp{q}"])

    def drain(units):
        while units:
            units.pop(0)()

    def run_units(units, nleft):
        if not units:
            return
        k = -(-len(units) // max(nleft, 1))
        for _ in range(k):
            if units:
                units.pop(0)()

    def ffn(i, j, passes, extra):
        q = i * 3 + j
        jj = 0 if j == 0 else 1
        tasks = []
        for (kind, blocks, units) in passes:
            nch = FC if kind == "gu" else KC
            for oc in range(nch):
                tasks.append((kind, oc, blocks, units, nch - oc))
        handles = {}

        def issue(idx):
            kind, oc = tasks[idx][0], tasks[idx][1]
            if kind == "gu":
                handles[idx] = wload(d_wgu[i, jj, oc], 2 * KC * 128)
            else:
                handles[idx] = wload(d_wd[i, jj, oc], FC * 128)

        issue(0)
        issue(1)
        fence("BIG")
        for idx, (kind, oc, blocks, units, nleft) in enumerate(tasks):
            if idx + 2 < len(tasks):
                issue(idx + 2)
            slot, rkey = handles[idx]
            if kind == "gu":
                w = slot[:, 0:2 * KC * 128].rearrange("p (t k n) -> p t k n", t=2, k=KC)
                for (a, n, cond) in blocks:
                    ba = next_mm()
                    bb_ = next_mm()
                    for t, bank in ((0, ba), (1, bb_)):
                        for kc in range(KC):
                            P.op("pe", I.matmul(ps[bank][:, 0:n], w[:, t, kc, :], hT[:, kc, a:a + n],
                                                start=(kc == 0), stop=(kc == KC - 1)),
                                 reads=[rkey] + hk(kc, a, a + n), writes=[f"ps{bank}"])
                    s_ = st["sg"] % 2
                    st["sg"] += 1
                    P.op("act", I.activation(out=sg[s_][:, 0:n], in_=ps[ba][:, 0:n], func=AF.Silu),
                         reads=[f"ps{ba}"], writes=[f"sg{s_}"])
                    P.op("dve", I.tensor_tensor(out=actT[:, oc, a:a + n], in0=sg[s_][:, 0:n], in1=ps[bb_][:, 0:n], op=ALU.mult),
                         reads=[f"sg{s_}", f"ps{bb_}", "BIG"], writes=[f"a{oc}s{sg_}" for sg_ in segs(a, a + n)])
                extra()
            else:
                w = slot[:, 0:FC * 128].rearrange("p (k n) -> p k n", k=FC)
                for (a, n, cond) in blocks:
                    bank = next_mm()
                    for fc in range(FC):
                        P.op("pe", I.matmul(ps[bank][:, 0:n], w[:, fc, :], actT[:, fc, a:a + n],
                                            start=(fc == 0), stop=(fc == FC - 1)),
                             reads=[rkey, "BIG"] + [f"a{fc}s{s_}" for s_ in segs(a, a + n)], writes=[f"ps{bank}"])
                    evac_z(q, oc, bank, a, a, n, cond)
            run_units(units, nleft if kind == "d" else max(nleft - 6, 1))

    def evac_z(q, oc, bank, pa, xa, n, cond):
        P.op("dve", I.scalar_tensor_tensor(
            out=xT[:, oc, xa:xa + n], in0=ps[bank][:, pa - pa:n], scalar=czT[:, q, oc, cond:cond + 1],
            in1=xT[:, oc, xa:xa + n], op0=ALU.mult, op1=ALU.add),
            reads=[f"ps{bank}", f"cz{q}"] + xk(oc, xa, xa + n), writes=xk(oc, xa, xa + n))

    def layer_norm(i, j, blocks, hmode, as_units=False):
        q = i * 3 + j

        def stats(a, n, cond):
            for c in range(KC):
                s = st["zb"] % 2
                st["zb"] += 1
                P.op("act", I.activation(out=zb[s][:, 0:n], in_=xT[:, c, a:a + n], func=AF.Copy),
                     reads=xk(c, a, a + n), writes=[f"zb{s}"])
                P.op("act", I.activation(out=sqb[s][:, 0:n], in_=xT[:, c, a:a + n], func=AF.Square),
                     reads=xk(c, a, a + n), writes=[f"sqb{s}"])
                P.op("pe", I.matmul(ps[4][:, 0:n], onesm, zb[s][:, 0:n], start=(c == 0), stop=(c == KC - 1)),
                     reads=["onesm", f"zb{s}"], writes=["ps4"])
                P.op("pe", I.matmul(ps[5][:, 0:n], onesm, sqb[s][:, 0:n], start=(c == 0), stop=(c == KC - 1)),
                     reads=["onesm", f"sqb{s}"], writes=["ps5"])

        def finalize(a, n, cond):
            P.op("act", I.activation(out=lnm[:, 0:n], in_=ps[4][:, 0:n], func=AF.Copy),
                 reads=["ps4"], writes=["lnm"])
            P.op("act", I.activation(out=lnv[:, 0:n], in_=ps[4][:, 0:n], func=AF.Square),
                 reads=["ps4"], writes=["lnv"])
            P.op("dve", I.scalar_tensor_tensor(out=lnv[:, 0:n], in0=lnv[:, 0:n], scalar=-1.0, in1=ps[5][:, 0:n],
                                               op0=ALU.mult, op1=ALU.add),
                 reads=["ps5", "lnv"], writes=["lnv"])
            P.op("act", I.activation(out=lnv[:, 0:n], in_=lnv[:, 0:n], func=AF.Ln, bias=epsb, scale=1.0),
                 reads=["lnv", "epsb"], writes=["lnv"])
            P.op("act", I.activation(out=lnv[:, 0:n], in_=lnv[:, 0:n], func=AF.Exp, scale=-0.5),
                 reads=["lnv"], writes=["lnv"])

        def apply(a, n, cond):
            for c in range(KC):
                xs = xT[:, c, a:a + n]
                keys = xk(c, a, a + n)
                P.op("dve", I.tensor_tensor(out=xs, in0=xs, in1=lnm[:, 0:n], op=ALU.subtract),
                     reads=keys + ["lnm"], writes=keys)
            for c in range(KC):
                xs = xT[:, c, a:a + n]
                keys = xk(c, a, a + n)
                P.op("dve", I.tensor_tensor(out=xs, in0=xs, in1=lnv[:, 0:n], op=ALU.mult),
                     reads=keys + ["lnv"], writes=keys)
            for c in range(KC):
                xs = xT[:, c, a:a + n]
                keys = xk(c, a, a + n)
                if hmode == "h":
                    P.op("act", I.activation(
                        out=hT[:, c, a:a + n], in_=xs, func=AF.Identity,
                        bias=BpT[:, q, c, cond:cond + 1], scale=GpT[:, q, c, cond:cond + 1]),
                        reads=keys + [f"Gp{q}", f"Bp{q}"], writes=hk(c, a, a + n))
                elif hmode == "hp":
                    pieces = [(a, n)] if a >= 512 else [(0, 256), (256, 256)]
                    for (pa, pn) in pieces:
                        hp0 = hpcol(pa)
                        P.op("act", I.activation(
                            out=hpT[:, c, hp0:hp0 + pn], in_=xT[:, c, pa:pa + pn], func=AF.Identity,
                            bias=BpT[:, q, c, cond:cond + 1], scale=GpT[:, q, c, cond:cond + 1]),
                            reads=keys + [f"Gp{q}", f"Bp{q}", "BIG"], writes=[f"hp{c}"])
            for c in range(KC):
                xs = xT[:, c, a:a + n]
                keys = xk(c, a, a + n)
                if c % 2 == 0:
                    P.op("dve", I.tensor_scalar(
                        out=xs, in0=xs, scalar1=lng[:, i, j, c:c + 1], scalar2=lnb[:, i, j, c:c + 1],
                        op0=ALU.mult, op1=ALU.add),
                        reads=keys + ["lng", "lnb"], writes=keys)
                else:
                    P.op("act", I.activation(
                        out=xs, in_=xs, func=AF.Identity,
                        bias=lnb[:, i, j, c:c + 1], scale=lng[:, i, j, c:c + 1]),
                        reads=keys + ["lng", "lnb"], writes=keys)

        if as_units:
            units = []
            for blk in blocks:
                units.append(lambda blk=blk: stats(*blk))
                units.append(lambda blk=blk: finalize(*blk))
                units.append(lambda blk=blk: apply(*blk))
            return units
        stats(*blocks[0])
        for bi, blk in enumerate(blocks):
            finalize(*blk)
            if bi + 1 < len(blocks):
                stats(*blocks[bi + 1])
            apply(*blk)
        return []

    o = 0

    def carve(nelem_bf16):
        nonlocal o
        v = BIG[:, o:o + nelem_bf16]
        o += nelem_bf16
        return v
    QW = 512 + MN
    qT = carve(KC * QW).rearrange("p (c n) -> p c n", c=KC)
    kT = carve(4 * NT).rearrange("p (g n) -> p g n", g=4)
    va0 = carve(17 * 4 * 68).rearrange("p (k g n) -> p k g n", k=17, g=4)
    va1 = carve(17 * 4 * 128).rearrange("p (k g n) -> p k g n", k=17, g=4)
    kcT = carve(4 * 256).rearrange("p (g n) -> p g n", g=4)
    cosT = carve(2 * SFN).bitcast(F32)
    sinT = carve(2 * SFN).bitcast(F32)
    assert o <= FC * NT, o
    o = 0
    hpT = carve(2 * KC * HPW).bitcast(F32).rearrange("p (c n) -> p c n", c=KC)
    Sa = carve(2 * 2 * HPW).bitcast(F32).rearrange("p (c n) -> p c n", c=2)
    Sb = carve(2 * 2 * HPW).bitcast(F32).rearrange("p (c n) -> p c n", c=2)
    assert o <= FC * NT, o

    def attention(wo_units=None, pre_units=None):
        q = 1
        fence("BIG")
        B = ["BIG"]
        sp_load(cosT, d_cos, "cosT", reads=B)
        sp_load(sinT, d_sin, "sinT", reads=B)
        P.op("pool", I.dma_start(out=kcT.rearrange("p g n -> p (g n)"), in_=d_ck, max_dma_last_dim=4096),
             reads=B, writes=["kcT"], stream="kcT")
        P.op("pool", I.dma_start(out=masks.rearrange("p a b -> p (a b)"), in_=d_masks, max_dma_last_dim=4096),
             writes=["masks"], stream="masks")
        P.op("act", I.activation(out=esink, in_=esink, func=AF.Exp), reads=["esink"], writes=["esink"])
        P.op("pool", I.memset(va0.rearrange("p k g n -> p (k g n)"), 0.0), reads=B, writes=["va0", "cosT", "sinT"][:1])
        P.op("pool", I.memset(va1.rearrange("p k g n -> p (k g n)"), 0.0), reads=B, writes=["va1"])
        for kci in range(17):
            if 4 <= kci < 15:
                src = kvalid[:, kci - 4:kci - 3].unsqueeze(1).broadcast_to([128, 4, 1])
                rk = ["kvalid"]
            else:
                src = onesf[:, 0:1].unsqueeze(1).broadcast_to([128, 4, 1])
                rk = ["onesf"]
            P.op("dve", (lambda kci, src: I.tensor_copy(out=va0[:, kci, :, 64:65], in_=src))(kci, src),
                 reads=rk + B, writes=["va0"])
            P.op("dve", (lambda kci, src: I.tensor_copy(out=va1[:, kci, :, 0:1], in_=src))(kci, src),
                 reads=rk + B, writes=["va1"])
        P.op("pool", I.dma_start(out=cvb.rearrange("p j g d -> p (j g d)"), in_=d_cv.rearrange("p j g d -> p (j g d)"),
                                 max_dma_last_dim=4096), writes=["cvb"], stream="cvb")
        for jx in range(2):
            P.op("act", (lambda jx: I.activation(out=va0[:, 15 + jx, :, 0:64], in_=cvb[:, jx], func=AF.Copy))(jx),
                 reads=["cvb"] + B, writes=["va0"])
            P.op("act", (lambda jx: I.activation(out=va1[:, 15 + jx, :, 64:128], in_=cvb[:, jx], func=AF.Copy))(jx),
                 reads=["cvb"] + B, writes=["va1"])

        if ATT_CUT <= 1:
            return
        ptasks = []

        def proj(oc_list, blocks, evac):
            ptasks.append((oc_list, blocks, evac))

        def run_ptasks(units):
            loaded = {}

            def load(t):
                if t < len(ptasks) and t not in loaded:
                    loaded[t] = [wload(d_wqk[oc], KC * 128) for oc in ptasks[t][0]]
            for t, (oc_list, blocks, evac) in enumerate(ptasks):
                load(t)
                if t + 1 < len(ptasks) and len(oc_list) + len(ptasks[t + 1][0]) <= NRING:
                    load(t + 1)
                slots = [(sl[:, 0:KC * 128].rearrange("p (k n) -> p k n", k=KC), rk) for (sl, rk) in loaded[t]]
                for (a, n, cond) in blocks:
                    banks = []
                    for (w, rkey) in slots:
                        bank = next_mm()
                        banks.append(bank)
                        for kc in range(KC):
                            P.op("pe", I.matmul(ps[bank][:, 0:n], w[:, kc, :], hT[:, kc, a:a + n],
                                                start=(kc == 0), stop=(kc == KC - 1)),
                                 reads=[rkey] + hk(kc, a, a + n), writes=[f"ps{bank}"])
                    evac(banks, a, n)
                if t < 12:
                    run_units(units, 12 - t)

        PB = [(0, 512, 0)]
        SFB = [(512, 432, 1), (944, 432, 1), (1376, 432, 1)]
        SMB = [(640, 352, 1), (992, 352, 1), (1344, 336, 1)]
        for c in range(KC):
            def ev(banks, a, n, c=c):
                P.op("act", I.activation(out=qT[:, c, a:a + n], in_=ps[banks[0]][:, 0:n], func=AF.Copy),
                     reads=[f"ps{banks[0]}"] + B, writes=[f"qT{c}"])
            proj([c], PB, ev)
        for g in range(4):
            def ev(banks, a, n, g=g):
                P.op("act", I.activation(out=kT[:, g, a:a + n], in_=ps[banks[0]][:, 0:n], func=AF.Copy),
                     reads=[f"ps{banks[0]}"] + B, writes=[f"kT{g}"])
                buf, bk = (sg[g % 2], f"sg{g % 2}")
                P.op("dve", I.tensor_copy(out=buf[:, 0:n], in_=ps[banks[0]][:, 0:n]),
                     reads=[f"ps{banks[0]}", f"kT{g}"], writes=[bk])
                P.op("sp", I.dma_start(out=d_sk[:, g, :], in_=buf[0:64, 0:n]),
                     reads=[bk], stream=f"sk{g}", final=True)
            proj([16 + g], PB, ev)

        def rope_ev(dst_fn, wkeys, toff):
            def ev(banks, a, n):
                ca = a - SF0
                P.op("dve", I.tensor_tensor(out=lnm[:, 0:n], in0=ps[banks[0]][:, 0:n], in1=cosT[:, ca:ca + n], op=ALU.mult),
                     reads=[f"ps{banks[0]}", "cosT"] + B, writes=["lnm"])
                P.op("dve", I.tensor_tensor(out=lnv[:, 0:n], in0=ps[banks[1]][:, 0:n], in1=sinT[:, ca:ca + n], op=ALU.mult),
                     reads=[f"ps{banks[1]}", "sinT"] + B, writes=["lnv"])
                P.op("dve", I.tensor_tensor(out=dst_fn(a, n), in0=lnm[:, 0:n], in1=lnv[:, 0:n], op=ALU.add),
                     reads=["lnm", "lnv"] + B, writes=wkeys)
            return ev
        for c in range(KC):
            proj([c, 8 + c], SMB, rope_ev(lambda a, n, c=c: qT[:, c, 512 + a - M0: 512 + a - M0 + n], [f"qT{c}"], 0))
        for g in range(4):
            proj([16 + g, 20 + g], SFB, rope_ev(lambda a, n, g=g: kT[:, g, a:a + n], [f"kT{g}"], 0))

        run_ptasks(pre_units if pre_units is not None else [])
        drain(pre_units if pre_units is not None else [])
        slot, rkey = wload(d_wv, KC * 256)
        wv = slot[:, 0:KC * 256].rearrange("p (k n) -> p k n", k=KC)
        for kci in range(15):
            a = kci * 128
            nk = 128 if kci < 14 else 16
            bank = next_mm()
            for kc in range(KC):
                P.op("pe", (lambda bank, kc, a, nk: I.matmul(
                    ps[bank][0:nk, 0:256], hT[:, kc, a:a + nk], wv[:, kc, :],
                    start=(kc == 0), stop=(kc == KC - 1)))(bank, kc, a, nk),
                    reads=[rkey] + hk(kc, a, a + nk), writes=[f"ps{bank}"])
            src = ps[bank][0:nk, 0:256].rearrange("p (g d) -> p g d", g=4)
            if kci < 4:
                P.op("act", (lambda kci, src, nk: I.activation(out=va0[0:nk, kci, :, 0:64], in_=src, func=AF.Copy))(kci, src, nk),
                     reads=[f"ps{bank}"] + B, writes=["va0"])
                P.op("act", (lambda kci, src, nk: I.activation(out=va1[0:nk, kci, :, 64:128], in_=src, func=AF.Copy))(kci, src, nk),
                     reads=[f"ps{bank}"] + B, writes=["va1"])
                buf, bk = (sg[kci % 2], f"sg{kci % 2}")
                P.op("dve", (lambda buf, bank: I.tensor_copy(out=buf[:, 0:256], in_=ps[bank][:, 0:256]))(buf, bank),
                     reads=[f"ps{bank}", "va1"], writes=[bk])
                P.op("sp", (lambda buf, kci: I.dma_start(out=d_sv[:, kci, :], in_=buf[:, 0:256]))(buf, kci),
                     reads=[bk], stream=f"sv{kci}", final=True)
            else:
                kv = kvalid[0:nk, kci - 4:kci - 3]
                P.op("act", (lambda kci, src, nk, kv: I.activation(out=va0[0:nk, kci, :, 0:64], in_=src, func=AF.Copy, scale=kv))(kci, src, nk, kv),
                     reads=[f"ps{bank}", "kvalid"] + B, writes=["va0"])
                P.op("act", (lambda kci, src, nk, kv: I.activation(out=va1[0:nk, kci, :, 64:128], in_=src, func=AF.Copy, scale=kv))(kci, src, nk, kv),
                     reads=[f"ps{bank}", "kvalid"] + B, writes=["va1"])

        if ATT_CUT <= 3:
            return
        pairs = []
        for b in range(2):
            for qt in range(2):
                kch = [(2 * b + jx, (lambda g, jx=jx, b=b: kT[:, g, b * 256 + jx * 128: b * 256 + jx * 128 + 128]), 128, None)
                       for jx in range(2)]
                for g in range(4):
                    pairs.append((b * 256 + qt * 128, 128, kch, b * 256 + qt * 128, g))
        if ATT_CUT > 4:
            for ti in range(9):
                nq = 128 if ti < 8 else 16
                kch = []
                for r in range(3):
                    ci = ti + r
                    nk = 128 if ci < 10 else 16
                    mid = 0 if r == 0 else (1 if r == 2 else None)
                    kch.append((4 + ci, (lambda g, ci=ci, nk=nk: kT[:, g, SF0 + ci * 128: SF0 + ci * 128 + nk]), nk, mid))
                for jx in range(2):
                    kch.append((15 + jx, (lambda g, jx=jx: kcT[:, g, jx * 128:(jx + 1) * 128]), 128, None))
                for g in range(4):
                    pairs.append((512 + ti * 128, nq, kch, M0 + ti * 128, g))

        def make_tail(nq, ocol, g, po, po1):
            d0 = dsb[64:65, 0:2 * nq]
            d1 = dsb[0:1, 2 * nq:4 * nq]
            stt = {}

            def part1():
                es0 = esink[64:65, 4 * g:4 * g + 2].unsqueeze(2).broadcast_to([1, 2, nq])
                es1 = esink[0:1, 4 * g + 2:4 * g + 4].unsqueeze(2).broadcast_to([1, 2, nq])
                P.op("dve", I.tensor_tensor(out=d0.rearrange("p (c n) -> p c n", c=2),
                                            in0=ps[po][64:65, 0:2 * nq].rearrange("p (c n) -> p c n", c=2),
                                            in1=es0, op=ALU.add),
                     reads=[f"ps{po}", "esink"], writes=["dsb0"])
                P.op("dve", I.tensor_tensor(out=d1.rearrange("p (c n) -> p c n", c=2),
                                            in0=ps[po1][0:1, 2 * nq:4 * nq].rearrange("p (c n) -> p c n", c=2),
                                            in1=es1, op=ALU.add),
                     reads=[f"ps{po1}", "esink"], writes=["dsb1"])
                P.op("act", I.activation(out=d0, in_=d0, func=AF.Ln), reads=["dsb0"], writes=["dsb0"])
                P.op("act", I.activation(out=d1, in_=d1, func=AF.Ln), reads=["dsb1"], writes=["dsb1"])
                P.op("act", I.activation(out=d0, in_=d0, func=AF.Exp, scale=-1.0), reads=["dsb0"], writes=["dsb0"])
                P.op("act", I.activation(out=d1, in_=d1, func=AF.Exp, scale=-1.0), reads=["dsb1"], writes=["dsb1"])
                bA, bB = po1, po
                stt["b"] = (bA, bB)
                P.op("pe", I.matmul(ps[bA][:, 0:2 * nq], onesf[64:65, 0:128], d0, start=True, stop=True),
                     reads=["dsb0", "onesf"], writes=[f"ps{bA}"])
                P.op("pe", I.matmul(ps[bB][:, 2 * nq:4 * nq], onesf[0:1, 0:128], d1, start=True, stop=True),
                     reads=["dsb1", "onesf"], writes=[f"ps{bB}"])

            def part2():
                bA, bB = stt["b"]
                s2 = st["sg"] % 2
                st["sg"] += 1
                P.op("dve", I.tensor_copy(out=sg[s2][:, 0:2 * nq], in_=ps[bA][:, 0:2 * nq]),
                     reads=[f"ps{bA}"], writes=[f"sg{s2}"])
                P.op("dve", I.tensor_copy(out=sg[s2][:, 2 * nq:4 * nq], in_=ps[bB][:, 2 * nq:4 * nq]),
                     reads=[f"ps{bB}"], writes=[f"sg{s2}"])
                ok = hk(2 * g, ocol, ocol + nq) + hk(2 * g + 1, ocol, ocol + nq)
                P.op("dve", I.tensor_tensor(
                    out=hT[0:64, 2 * g:2 * g + 2, ocol:ocol + nq],
                    in0=ps[po][0:64, 0:2 * nq].rearrange("p (c n) -> p c n", c=2),
                    in1=sg[s2][0:64, 0:2 * nq].rearrange("p (c n) -> p c n", c=2), op=ALU.mult),
                    reads=[f"ps{po}", f"sg{s2}"], writes=ok)
                P.op("dve", I.tensor_tensor(
                    out=hT[64:128, 2 * g:2 * g + 2, ocol:ocol + nq],
                    in0=ps[po1][64:128, 2 * nq:4 * nq].rearrange("p (c n) -> p c n", c=2),
                    in1=sg[s2][64:128, 2 * nq:4 * nq].rearrange("p (c n) -> p c n", c=2), op=ALU.mult),
                    reads=[f"ps{po1}", f"sg{s2}"], writes=ok)
            return part1, part2

        prev_tail = None
        for pi, (qc0, nq, kchunks, ocol, g) in enumerate(pairs):
            po, po1 = (4, 5) if pi % 2 == 0 else (6, 7)
            nkc = len(kchunks)
            slots = {}
            sbanks = {}

            def score(ki):
                kci, ksrc, nk, mid = kchunks[ki]
                kg = ksrc(g)
                slots[ki] = st["pT"] % 3
                st["pT"] += 1
                bl = []
                for half in range(2):
                    bank = next_mm()
                    bl.append(bank)
                    p0 = half * 64
                    P.op("pe", I.matmul(
                        ps[bank][0:nk, 0:2 * nq].rearrange("p (c n) -> p c n", c=2),
                        kg[p0:p0 + 64, 0:nk], qT[p0:p0 + 64, 2 * g:2 * g + 2, qc0:qc0 + nq],
                        start=True, stop=True),
                        reads=[f"kT{g}", "kcT", f"qT{2 * g}", f"qT{2 * g + 1}"] + B, writes=[f"ps{bank}"])
                sbanks[ki] = bl

            def exp_pv(ki, between=None):
                kci, ksrc, nk, mid = kchunks[ki]
                s = slots[ki]
                for half in range(2):
                    bank = sbanks[ki][half]
                    P.op("act", I.activation(
                        out=pT[s][0:nk, half * 2 * nq:(half + 1) * 2 * nq], in_=ps[bank][0:nk, 0:2 * nq],
                        func=AF.Exp, scale=ATT_SCALE),
                        reads=[f"ps{bank}"], writes=[f"pT{s}h{half}"])
                if mid is not None:
                    pv = pT[s][0:nk, 0:4 * nq].rearrange("p (c n) -> p c n", c=4)
                    mk = masks[0:nk, mid, 0:nq].unsqueeze(1).broadcast_to([nk, 4, nq])
                    P.op("dve", I.tensor_tensor(out=pv, in0=pv, in1=mk, op=ALU.mult),
                         reads=[f"pT{s}h0", f"pT{s}h1", "masks"], writes=[f"pT{s}h0", f"pT{s}h1"])
                if between is not None:
                    between()
                P.op("pe", I.matmul(
                    ps[po][0:65, 0:2 * nq], va0[0:nk, kci, g, 0:65], pT[s][0:nk, 0:2 * nq],
                    start=(ki == 0), stop=(ki == nkc - 1)),
                    reads=[f"pT{s}h0", "va0"] + B, writes=[f"ps{po}"])
                P.op("pe", I.matmul(
                    ps[po1][:, 2 * nq:4 * nq], va1[0:nk, kci, g, :], pT[s][0:nk, 2 * nq:4 * nq],
                    start=(ki == 0), stop=(ki == nkc - 1)),
                    reads=[f"pT{s}h1", "va1"] + B, writes=[f"ps{po1}"])

            score(0)
            if nkc > 1:
                score(1)
            done1 = done2 = False
            for ki in range(nkc):
                exp_pv(ki, (lambda ki=ki: score(ki + 2)) if ki + 2 < nkc else None)
                if prev_tail is not None:
                    if ki == min(1, nkc - 1) and not done1:
                        prev_tail[0]()
                        done1 = True
                    if ki == min(3, nkc - 1) and done1 and not done2 and ki >= 1:
                        prev_tail[1]()
                        done2 = True
            if prev_tail is not None and not done2:
                if not done1:
                    prev_tail[0]()
                prev_tail[1]()
            prev_tail = make_tail(nq, ocol, g, po, po1)
        prev_tail[0]()
        prev_tail[1]()

        if ATT_CUT <= 5:
            return
        wo_units = wo_units if wo_units is not None else []
        for pi_, pblocks in enumerate((MAIN_BLOCKS[:2], MAIN_BLOCKS[2:])):
            for oc in range(KC):
                slot, rkey = wload(d_wo[oc], KC * 128)
                w = slot[:, 0:KC * 128].rearrange("p (k n) -> p k n", k=KC)
                for (a, n, cond) in pblocks:
                    bank = next_mm()
                    for kc in range(KC):
                        P.op("pe", I.matmul(ps[bank][:, 0:n], w[:, kc, :], hT[:, kc, a:a + n],
                                            start=(kc == 0), stop=(kc == KC - 1)),
                             reads=[rkey] + hk(kc, a, a + n), writes=[f"ps{bank}"])
                    evac_z(q, oc, bank, a, a, n, cond)
                if pi_ == 1:
                    run_units(wo_units, KC - oc)

    def pool_prepare():
        fence("BIG")
        P.op("pool", I.memset(hpT.rearrange("p c n -> p (c n)"), 0.0), reads=["BIG"],
             writes=[f"hp{c}" for c in range(KC)])

    def pool_mix():
        q = 4
        B = ["BIG"]
        allhp = [f"hp{c}" for c in range(KC)]
        for (col, pv0) in ((536, 0), (1568, 8)):
            P.op("dve", (lambda col, pv0: I.tensor_tensor(
                out=hpT[:, :, col:col + 8], in0=hpT[:, :, col:col + 8],
                in1=pvalid[:, pv0:pv0 + 8].unsqueeze(1).broadcast_to([128, KC, 8]), op=ALU.mult))(col, pv0),
                reads=allhp + ["pvalid"] + B, writes=allhp)
        W = HPW
        edges = []
        for (hp, xa, nn) in HP_SEGS:
            if nn == 256:
                edges.append((hp, xa))
                edges.append((hp + nn - 8, xa + nn - 8))
            else:
                edges.append((hp + 8, xa + 8))
                edges.append((hp + nn - 16, xa + nn - 16))
        for gi in range(4):
            eng = "dve"
            hsl = hpT[:, 2 * gi:2 * gi + 2, :]
            hkeys = [f"hp{2 * gi}", f"hp{2 * gi + 1}"]
            bufs = [(Sa, "Sa"), (Sb, "Sb")]
            cur, ck = hsl, hkeys
            steps = [(1, 0, 1), (1, 1, 2), (2, 2, 4), (4, 4, 8)][:gi + 1]
            lo, hi = 0, W
            for si, (ls, rs, _) in enumerate(steps):
                dst, dk = bufs[si % 2]
                nlo, nhi = lo + ls, hi - rs
                P.op(eng, (lambda dst, cur, nlo, nhi, ls, rs: I.tensor_tensor(
                    out=dst[:, :, nlo:nhi], in0=cur[:, :, nlo - ls:nhi - ls], in1=cur[:, :, nlo + rs:nhi + rs], op=ALU.add))(dst, cur, nlo, nhi, ls, rs),
                    reads=ck + B, writes=[dk])
                cur, ck = dst, [dk]
                lo, hi = nlo, nhi
            w = POOL_W[gi]
            for (hp, xa, nn) in HP_SEGS:
                okeys = hk(2 * gi, xa, xa + nn) + hk(2 * gi + 1, xa, xa + nn)
                P.op("dve", (lambda cur, hp, xa, nn: I.scalar_tensor_tensor(
                    out=hT[:, 2 * gi:2 * gi + 2, xa:xa + nn], in0=cur[:, :, hp:hp + nn], scalar=1.0 / w,
                    in1=hsl[:, :, hp:hp + nn], op0=ALU.mult, op1=ALU.subtract))(cur, hp, xa, nn),
                    reads=ck + hkeys + B, writes=okeys)
            for ei, (hp, xa) in enumerate(edges):
                okeys = hk(2 * gi, xa, xa + 8) + hk(2 * gi + 1, xa, xa + 8)
                pv = pinv[:, gi, ei, :].unsqueeze(1).broadcast_to([128, 2, 8])
                et = etmp[eng]
                P.op(eng, (lambda cur, hp, pv, et: I.tensor_tensor(
                    out=et, in0=cur[:, :, hp:hp + 8], in1=pv, op=ALU.mult))(cur, hp, pv, et),
                    reads=ck + ["pinv"] + B, writes=["etmp" + eng])
                P.op(eng, (lambda hp, xa, et: I.tensor_tensor(
                    out=hT[:, 2 * gi:2 * gi + 2, xa:xa + 8], in0=et,
                    in1=hsl[:, :, hp:hp + 8], op=ALU.subtract))(hp, xa, et),
                    reads=["etmp" + eng] + hkeys + B, writes=okeys)
        for oc in range(KC):
            gi = oc // 2
            slot, rkey = wload(d_poolw[oc], 2 * 128)
            w2 = slot[:, 0:256].rearrange("p (k n) -> p k n", k=2)
            for (a, n, cond) in MAIN_BLOCKS:
                bank = next_mm()
                for kk in range(2):
                    P.op("pe", (lambda bank, kk, a, n, w2, gi: I.matmul(
                        ps[bank][:, 0:n], w2[:, kk, :], hT[:, 2 * gi + kk, a:a + n],
                        start=(kk == 0), stop=(kk == 1)))(bank, kk, a, n, w2, gi),
                        reads=[rkey] + hk(2 * gi + kk, a, a + n), writes=[f"ps{bank}"])
                evac_z(q, oc, bank, a, a, n, cond)

    pending = []

    def extra(npieces=3):
        ada_compute()
        for _ in range(npieces):
            if pending:
                ada_issue(*pending.pop(0))

    def flush():
        ada_compute()
        while pending:
            for _ in range(len(ada_slots)):
                if pending:
                    ada_issue(*pending.pop(0))
            ada_compute()

    emit_adaln(0, 0)
    emit_adaln(0, 1)
    P.op("dve", I.tensor_scalar(out=S0T, in0=modv(0, 1), scalar1=1.0, scalar2=None, op0=ALU.add),
         reads=["mod0_1"], writes=["S0T"])
    for (a, n, cond) in FULL_BLOCKS:
        for c in range(KC):
            P.op("act", (lambda a, n, cond, c: I.activation(
                out=hT[:, c, a:a + n], in_=xT[:, c, a:a + n], func=AF.Identity,
                bias=modT[:, 0, 0 * 8 + c, cond:cond + 1], scale=S0T[:, c, cond:cond + 1]))(a, n, cond, c),
                reads=xk(c, a, a + n) + ["S0T", "mod0_0"], writes=hk(c, a, a + n))
    pending += [(0, m, c) for m in range(2, 9) for c in range(KC)]
    stage = 0

    def done():
        nonlocal stage
        stage += 1
        return stage >= upto

    def finish():
        for c in range(KC):
            P.op("sp", I.dma_start(out=d_y[:, c, 0:512], in_=xT[:, c, 0:512]),
                 reads=xk(c, 0, 512), stream=f"y{c}a", final=True)
        for (lo, hi) in ((OWN0, 992), (992, 1344), (1344, OWN0 + OWNN)):
            for c in range(KC):
                P.op("sp", I.dma_start(out=d_y[:, c, 512 + lo - OWN0:512 + hi - OWN0], in_=xT[:, c, lo:hi]),
                     reads=xk(c, lo, hi), stream=f"y{c}b{lo}", final=True)

    def run():
        FA, FB = FULL_BLOCKS[:2], FULL_BLOCKS[2:]
        MA, MB = MAIN_BLOCKS[:2], MAIN_BLOCKS[2:]
        uA = layer_norm(0, 0, FA, "h", as_units=True)
        ffn(0, 0, [("gu", FULL_BLOCKS, []), ("d", FA, []), ("d", FB, uA)], extra_with_coefs(0, 0, FC))
        drain(uA)
        uB0 = layer_norm(0, 0, FB, "h", as_units=True)
        emit_coefs(0, 1)
        pending.extend([(1, m, c) for m in range(9) for c in range(KC)])
        uA = layer_norm(0, 1, MA, "h", as_units=True)
        attention(uA, uB0)
        drain(uA)
        uB = layer_norm(0, 1, MB, "h", as_units=True)
        uA = layer_norm(0, 2, MA, "h", as_units=True)
        ffn(0, 2, [("gu", MA, uB), ("gu", MB, []), ("d", MA, []), ("d", MB, uA)], extra_with_coefs(0, 2, 2 * FC))
        drain(uA)
        uB = layer_norm(0, 2, MB, "h", as_units=True)
        emit_coefs(1, 0)
        ffn(1, 0, [("gu", MA, uB), ("gu", MB, []), ("d", MA, []), ("d", MB, [])], lambda: None)
        emit_coefs(1, 1)
        pool_prepare()
        layer_norm(1, 0, MAIN_BLOCKS, "hp")
        pool_mix()
        emit_coefs(1, 2)
        layer_norm(1, 1, MA, "h")
        uB = layer_norm(1, 1, MB, "h", as_units=True)
        uA = layer_norm(1, 2, MA, None, as_units=True)
        uB1 = layer_norm(1, 2, MB[:1], None, as_units=True)
        ffn(1, 2, [("gu", MA, uB), ("gu", MB, []), ("d", MA, []), ("d", MB[:1], uA), ("d", MB[1:], uB1)], lambda: None)
        drain(uA)
        drain(uB1)
        layer_norm(1, 2, MB[1:], None)

    def extra_with_coefs(i, j, total):
        state = {"n": 0}

        def f():
            state["n"] += 1
            extra(3 if (i, j) == (0, 0) else 2)
            if state["n"] == total:
                flush()
                emit_coefs(i, j)
        return f

    run()
    finish()

    with ExitStack() as es:
        sems = {n: es.enter_context(nc.semaphore(n)) for n in P.sem_names()}
        with nc.Block() as block:
            P.emit(block, sems)
    return nc


def _fm(x2d):
    T = x2d.shape[0]
    return np.ascontiguousarray(x2d.reshape(T, KC, 128).transpose(2, 1, 0))


def _wchunks(w, kc, noc):
    return np.ascontiguousarray(w.reshape(kc, 128, noc, 128).transpose(2, 1, 0, 3).reshape(noc, 128, kc * 128))


def prepare_inputs(x_prompt, x_sample, cache_k, cache_v, c, c_ctx, w_mod, b_mod, ln_g, ln_b,
                   ffn_w_gate, ffn_w_up, ffn_w_down, attn_w_qkv, attn_w_o, attn_sink, pool_w, pool_scale):
    f32 = np.float32
    shared = {}
    shared["wmod"] = np.stack([_wchunks(w_mod[i], KC, 72) for i in range(2)])
    shared["bmodT"] = np.ascontiguousarray(b_mod.reshape(2, 72, 128).transpose(2, 0, 1))
    shared["lngT"] = np.ascontiguousarray(ln_g.reshape(2, 3, KC, 128).transpose(3, 0, 1, 2))
    shared["lnbT"] = np.ascontiguousarray(ln_b.reshape(2, 3, KC, 128).transpose(3, 0, 1, 2))
    wgu = np.empty((2, 2, FC, 128, 2, KC * 128), f32)
    wd = np.empty((2, 2, KC, 128, FC * 128), f32)
    for i in range(2):
        for j in range(2):
            wgu[i, j, :, :, 0, :] = _wchunks(ffn_w_gate[i, j], KC, FC)
            wgu[i, j, :, :, 1, :] = _wchunks(ffn_w_up[i, j], KC, FC)
            wd[i, j] = _wchunks(ffn_w_down[i, j], FC, KC)
    shared["wgu"] = wgu.reshape(2, 2, FC, 128, 2 * KC * 128)
    shared["wd"] = wd
    wqkv = attn_w_qkv[0]
    perm = np.concatenate([np.arange(16, 32), np.arange(0, 16), np.arange(48, 64), np.arange(32, 48)])
    wq = wqkv[:, :1024]
    wk = wqkv[:, 1024:1280]
    wvv = wqkv[:, 1280:1536]
    wqp = wq.reshape(1024, 16, 64)[:, :, perm].reshape(1024, 1024)
    wkd = np.repeat(wk.reshape(1024, 4, 1, 64), 2, axis=2).reshape(1024, 512)
    wkp = wk.reshape(1024, 4, 64)[:, :, perm]
    wkpd = np.repeat(wkp.reshape(1024, 4, 1, 64), 2, axis=2).reshape(1024, 512)
    wall = np.concatenate([wq, wqp, wkd, wkpd], axis=1)
    shared["wqk"] = _wchunks(wall, KC, 24)
    shared["wv"] = np.ascontiguousarray(wvv.reshape(KC, 128, 256).transpose(1, 0, 2).reshape(128, KC * 256))
    shared["wo"] = _wchunks(attn_w_o[0], KC, KC)
    pw = np.empty((KC, 128, 2 * 128), f32)
    for oc in range(KC):
        gi = oc // 2
        blk = pool_w[0, gi][:, (oc % 2) * 128:(oc % 2) * 128 + 128]
        pw[oc] = blk.reshape(2, 128, 128).transpose(1, 0, 2).reshape(128, 256)
    shared["poolw"] = pw
    shared["pscaleT"] = np.ascontiguousarray(pool_scale[0].reshape(KC, 128).T)
    sk_ord = attn_sink[0].reshape(4, 2, 2).transpose(0, 2, 1).reshape(16)
    shared["sinkb"] = np.ascontiguousarray(np.broadcast_to(sk_ord[None, :], (128, 16)))
    ql = np.arange(128)
    mk = np.empty((128, 2, 128), f32)
    mk[:, 0, :] = (ql[:, None] >= ql[None, :])
    mk[:, 1, :] = (ql[:, None] <= ql[None, :])
    shared["masks"] = mk.reshape(128, 256)

    inv = np.power(10000.0, -np.arange(16, dtype=np.float64) / 16.0)
    sign = np.concatenate([-np.ones(16), np.ones(16), -np.ones(16), np.ones(16)]).astype(f32)
    in_maps = []
    for core in range(8):
        sbi, qd = core // 4, core % 4
        s0 = qd * 1024
        m = dict(shared)
        xcols = np.zeros((NT, D), f32)
        xcols[0:256] = x_prompt[2 * core]
        xcols[256:512] = x_prompt[2 * core + 1]
        pos = s0 - HALO + np.arange(SFN)
        ok = (pos >= 0) & (pos < 4096)
        xcols[512:][ok] = x_sample[sbi][pos[ok]]
        m["xT"] = _fm(xcols)
        cond = np.stack([c_ctx, c[sbi]], axis=1)
        m["condT"] = np.ascontiguousarray(cond.reshape(KC, 128, 2).transpose(1, 0, 2))
        posc = np.clip(pos, 0, 4095)
        rows = (posc // 64).astype(np.float64)
        cols = (posc % 64).astype(np.float64)
        ang_r = rows[:, None] * inv[None, :]
        ang_c = cols[:, None] * inv[None, :]
        ang = np.concatenate([ang_r, ang_r, ang_c, ang_c], axis=1)
        cosv = np.cos(ang).astype(f32).T
        sinv = (np.sin(ang) * sign[None, :].astype(np.float64)).astype(f32).T
        m["ropecos"] = np.ascontiguousarray(np.concatenate([cosv, cosv], axis=0))
        m["ropesin"] = np.ascontiguousarray(np.concatenate([sinv, sinv], axis=0))
        ck = cache_k[sbi, 0]
        ckT = ck.transpose(1, 2, 0)
        ckT = np.concatenate([ckT, ckT], axis=1)
        m["ckT"] = np.ascontiguousarray(ckT.transpose(1, 0, 2).reshape(128, 4 * 256))
        m["cv"] = np.ascontiguousarray(cache_v[sbi, 0].reshape(2, 128, 4, 64).transpose(1, 0, 2, 3))
        kval = np.zeros((128, 11), f32)
        okp = np.zeros(11 * 128, f32)
        okp[:SFN] = ok
        kval[:, :] = okp.reshape(11, 128).T
        m["kvalid"] = kval
        pinv = np.ones((4, 6, 8), f32)
        for gi, w in enumerate(POOL_W):
            pinv[gi] = 1.0 / w
            tl = np.arange(8)
            cl = (np.minimum(tl + w // 2, 256) - np.maximum(tl - w // 2, 0)).astype(f32)
            tr = 248 + np.arange(8)
            cr = (np.minimum(tr + w // 2, 256) - np.maximum(tr - w // 2, 0)).astype(f32)
            for e in (0, 2):
                pinv[gi, e] = 1.0 / cl
                pinv[gi, e + 1] = 1.0 / cr
            pl = s0 + np.arange(8)
            pr = s0 + 1024 - 8 + np.arange(8)
            pinv[gi, 4] = 1.0 / (np.minimum(pl + w // 2, 4096) - np.maximum(pl - w // 2, 0)).astype(f32)
            pinv[gi, 5] = 1.0 / (np.minimum(pr + w // 2, 4096) - np.maximum(pr - w // 2, 0)).astype(f32)
        m["pinv"] = np.ascontiguousarray(np.broadcast_to(pinv.reshape(1, -1), (128, 4 * 6 * 8)))
        pv = np.ones(16, f32)
        pv[:8] = ((s0 - 8 + np.arange(8)) >= 0)
        pv[8:] = ((s0 + 1024 + np.arange(8)) < 4096)
        m["pvalid"] = np.ascontiguousarray(np.broadcast_to(pv[None, :], (128, 16)))
        in_maps.append({k: np.ascontiguousarray(v, dtype=f32) for k, v in m.items()})
    return in_maps


_NC_CACHE = {}


def kernel(**inputs):
    inputs = {k: np.asarray(v) for k, v in inputs.items()}
    in_maps = prepare_inputs(**inputs)
    if "nc" not in _NC_CACHE:
        _NC_CACHE["nc"] = build_program()
    nc = _NC_CACHE["nc"]
    res = run_bass_kernel_spmd(nc, in_maps, core_ids=list(range(8)))
    y_prompt = np.empty((16, 256, D), np.float32)
    y_sample = np.empty((2, 4096, D), np.float32)
    state_k = np.empty((16, 1, 256, 4, 64), np.float32)
    state_v = np.empty((16, 1, 256, 4, 64), np.float32)
    for core in range(8):
        r = res.results[core]
        sbi, qd = core // 4, core % 4
        y = r["yT"].transpose(2, 1, 0).reshape(1536, D)
        y_prompt[2 * core] = y[0:256]
        y_prompt[2 * core + 1] = y[256:512]
        y_sample[sbi, qd * 1024:(qd + 1) * 1024] = y[512:]
        sk = r["skT"].transpose(2, 1, 0)
        state_k[2 * core, 0] = sk[0:256]
        state_k[2 * core + 1, 0] = sk[256:512]
        sv = r["sv"].transpose(1, 0, 2).reshape(512, 4, 64)
        state_v[2 * core, 0] = sv[0:256]
        state_v[2 * core + 1, 0] = sv[256:512]
    return (y_prompt, y_sample, state_k, state_v)
```

```python
import math
from contextlib import ExitStack
import numpy as np
import concourse.bass as bass
import concourse.mybir as mybir
from concourse.bass_utils import run_bass_kernel_spmd

F32 = mybir.dt.float32
BF16 = mybir.dt.bfloat16
AF = mybir.ActivationFunctionType
ALU = mybir.AluOpType

ENGS = ("pe", "act", "dve", "pool", "sp")


class _Rec:
    def __getattr__(self, name):
        def f(*args, **kwargs):
            return (name, args, kwargs)
        return f


I = _Rec()


SAME_ENG_DIST = 10 ** 9
ATT_CUT = 99
LN_XNEW_ENG = "pool"


class Op:
    __slots__ = ("eng", "fn", "deps", "idx", "dma", "stream", "signal", "cum")

    def __init__(self, eng, fn, idx, dma, stream):
        self.eng = eng
        self.fn = fn
        self.deps = set()
        self.idx = idx
        self.dma = dma
        self.stream = stream
        self.signal = False
        self.cum = 0


class Prog:
    def __init__(self, nc, same_eng_dist=None):
        self.nc = nc
        self.eng_ops = {e: [] for e in ENGS}
        self.last_writer = {}
        self.readers = {}
        self.streams = {}
        self.same_eng_dist = SAME_ENG_DIST if same_eng_dist is None else same_eng_dist
        self.final_ops = []

    def op(self, eng, fn, reads=(), writes=(), stream=None, final=False):
        dma = stream is not None
        lst = self.eng_ops[eng]
        o = Op(eng, fn, len(lst), dma, stream)
        lst.append(o)
        for k in reads:
            w = self.last_writer.get(k)
            if w is not None:
                o.deps.add(w)
            self.readers.setdefault(k, []).append(o)
        for k in writes:
            w = self.last_writer.get(k)
            if w is not None:
                o.deps.add(w)
            for r in self.readers.get(k, ()):
                if r is not o:
                    o.deps.add(r)
            self.last_writer[k] = o
            self.readers[k] = []
        if dma:
            self.streams.setdefault(stream, []).append(o)
        if final:
            self.final_ops.append(o)
        return o

    def _prune(self, o):
        best = {}
        for d in o.deps:
            if d is o:
                continue
            if (not d.dma) and d.eng == o.eng:
                if o.eng == "pe":
                    continue
                if o.idx - d.idx >= self.same_eng_dist:
                    continue
            key = ("S_" + d.stream) if d.dma else ("E_" + d.eng)
            b = best.get(key)
            if b is None or d.idx > b.idx:
                best[key] = d
        return best

    def emit(self, block, sems):
        pruned = {}
        for e in ENGS:
            for o in self.eng_ops[e]:
                ds = self._prune(o)
                pruned[id(o)] = ds
                for d in ds.values():
                    if not d.dma:
                        d.signal = True
        for e in ENGS:
            c = 0
            for o in self.eng_ops[e]:
                if (not o.dma) and o.signal:
                    c += 1
                    o.cum = c
        for s, lst in self.streams.items():
            c = 0
            for o in lst:
                c += 16
                o.cum = c

        def emit_engine(eng_name, eng):
            waited = {}
            for o in self.eng_ops[eng_name]:
                for key, d in pruned[id(o)].items():
                    v = d.cum
                    if waited.get(key, 0) >= v:
                        continue
                    eng.wait_ge(sems[key], v)
                    waited[key] = v
                name, args, kwargs = o.fn
                ins = getattr(eng, name)(*args, **kwargs)
                if o.dma:
                    ins.then_inc(sems["S_" + o.stream], 16)
                elif o.signal:
                    ins.then_inc(sems["E_" + o.eng], 1)
            if eng_name == "sp":
                for o in self.final_ops:
                    key = "S_" + o.stream
                    v = self.streams[o.stream][-1].cum
                    if waited.get(key, 0) < v:
                        eng.wait_ge(sems[key], v)
                        waited[key] = v

        @block.tensor
        def _(eng):
            emit_engine("pe", eng)

        @block.scalar
        def _(eng):
            emit_engine("act", eng)

        @block.vector
        def _(eng):
            emit_engine("dve", eng)

        @block.gpsimd
        def _(eng):
            emit_engine("pool", eng)

        @block.sync
        def _(eng):
            emit_engine("sp", eng)

    def sem_names(self):
        return ["E_" + e for e in ENGS if e != "sp"] + ["S_" + s for s in self.streams]


D = 1024
KC = 8
DFF = 2816
FC = 22
NT = 1808
SF0 = 512
SFN = 1296
M0 = 640
MN = 1040
OWN0 = 648
OWNN = 1024
HALO = 136
ALPHA = 4.0 ** 0.25
EPS2 = 1e-5 / (ALPHA * ALPHA)
ATT_SCALE = 0.125
POOL_W = (2, 4, 8, 16)
SEG_BOUNDS = [0, 512, 640, 944, 992, 1344, 1376, 1680, 1808]
FULL_BLOCKS = [(0, 512, 0), (512, 432, 1), (944, 432, 1), (1376, 432, 1)]
MAIN_BLOCKS = [(0, 512, 0), (640, 352, 1), (992, 352, 1), (1344, 336, 1)]
HPW = 1584
HP_SEGS = [(8, 0, 256), (272, 256, 256), (536, 640, 1040)]
RING_ELEMS = 2816
NRING = 3


def hpcol(a):
    for (hp, xa, nn) in HP_SEGS:
        if xa <= a < xa + nn:
            return hp + (a - xa)
    raise ValueError(a)


def segs(a, b):
    out = []
    for i in range(len(SEG_BOUNDS) - 1):
        if SEG_BOUNDS[i] < b and SEG_BOUNDS[i + 1] > a:
            out.append(i)
    return out


def xk(c, a, b):
    return [f"x{c}s{s}" for s in segs(a, b)]


def hk(c, a, b):
    return [f"h{c}s{s}" for s in segs(a, b)]


def build_program(upto=99):
    nc = bass.Bass("TRN2", target_bir_lowering=False)
    dt_in = lambda name, shape: nc.dram_tensor(name, shape, F32, kind="ExternalInput").ap()
    d_xT = dt_in("xT", [128, KC, NT])
    d_cond = dt_in("condT", [128, KC, 2])
    d_wmod = dt_in("wmod", [2, 72, 128, KC * 128])
    d_bmod = dt_in("bmodT", [128, 2, 72])
    d_lng = dt_in("lngT", [128, 2, 3, KC])
    d_lnb = dt_in("lnbT", [128, 2, 3, KC])
    d_wgu = dt_in("wgu", [2, 2, FC, 128, 2 * KC * 128])
    d_wd = dt_in("wd", [2, 2, KC, 128, FC * 128])
    d_wqk = dt_in("wqk", [24, 128, KC * 128])
    d_wv = dt_in("wv", [128, KC * 256])
    d_wo = dt_in("wo", [KC, 128, KC * 128])
    d_poolw = dt_in("poolw", [KC, 128, 2 * 128])
    d_pscale = dt_in("pscaleT", [128, KC])
    d_sink = dt_in("sinkb", [128, 16])
    d_cos = dt_in("ropecos", [128, SFN])
    d_sin = dt_in("ropesin", [128, SFN])
    d_ck = dt_in("ckT", [128, 4 * 256])
    d_cv = dt_in("cv", [128, 2, 4, 64])
    d_masks = dt_in("masks", [128, 2 * 128])
    d_kvalid = dt_in("kvalid", [128, 11])
    d_pinv = dt_in("pinv", [128, 4 * 6 * 8])
    d_pvalid = dt_in("pvalid", [128, 16])
    d_y = nc.dram_tensor("yT", [128, KC, 1536], F32, kind="ExternalOutput").ap()
    d_sk = nc.dram_tensor("skT", [64, 4, 512], F32, kind="ExternalOutput").ap()
    d_sv = nc.dram_tensor("sv", [128, 4, 256], F32, kind="ExternalOutput").ap()

    sb = lambda name, shape, dt: nc.alloc_sbuf_tensor(name, shape, dt).ap()
    xT = sb("xT_sb", [128, KC, NT], F32)
    hT = sb("hT_sb", [128, KC, NT], BF16)
    BIG = sb("big", [128, FC * NT], BF16)
    actT = BIG.rearrange("p (f n) -> p f n", f=FC)
    ring = [sb(f"ring{i}", [128, RING_ELEMS], BF16) for i in range(NRING)]
    sg = [sb(f"sg{i}", [128, 512], F32) for i in range(2)]
    zb = [sb(f"zb{i}", [128, 512], BF16) for i in range(2)]
    sqb = [sb(f"sqb{i}", [128, 512], BF16) for i in range(2)]
    lnm = sb("lnm", [128, 512], F32)
    lnv = sb("lnv", [128, 512], F32)
    pT = [sb(f"pT{i}", [128, 512], BF16) for i in range(3)]
    condT = sb("cond_sb", [128, KC, 2], F32)
    scT = sb("scT", [128, KC, 2], BF16)
    modT = sb("modT", [128, 2, 72, 2], F32)
    bmodT = sb("bmod_sb", [128, 2, 72], F32)
    lng = sb("lng_sb", [128, 2, 3, KC], F32)
    lnb = sb("lnb_sb", [128, 2, 3, KC], F32)
    pscale = sb("pscale_sb", [128, KC], F32)
    czT = sb("czT", [128, 6, KC, 2], F32)
    GpT = sb("GpT", [128, 6, KC, 2], F32)
    BpT = sb("BpT", [128, 6, KC, 2], F32)
    tmpc = sb("tmpc", [128, KC, 2], F32)
    S0T = sb("S0T", [128, KC, 2], F32)
    onesm = sb("onesm", [128, 128], BF16)
    onesf = sb("onesf", [128, 128], F32)
    epsb = sb("epsb", [128, 1], F32)
    esink = sb("esink", [128, 16], F32)
    masks = sb("masks_sb", [128, 2, 128], BF16)
    kvalid = sb("kvalid_sb", [128, 11], F32)
    pinv = sb("pinv_sb", [128, 4, 6, 8], F32)
    pvalid = sb("pvalid_sb", [128, 16], F32)
    dsb = sb("dsb", [128, 512], F32)
    ar2 = sb("ar2", [128, KC * 128], BF16)
    ar3 = sb("ar3", [128, KC * 128], BF16)
    cvb = sb("cvb", [128, 2, 4, 64], BF16)
    dummy = sb("dummy_sb", [128, 8], F32)
    etmp = {"dve": sb("etmp_dve", [128, 2, 8], F32), "pool": sb("etmp_pool", [128, 2, 8], F32)}

    ps = [nc.alloc_psum_tensor(f"ps{i}", [128, 512], F32).ap() for i in range(8)]

    P = Prog(nc)
    st = {"ring": 0, "mm": 0, "sg": 0, "zb": 0, "pT": 0, "ld": 0, "ada": 0}

    def sp_load(dst, src, key, reads=()):
        st["ld"] += 1
        P.op("sp", I.dma_start(out=dst, in_=src), reads=list(reads), writes=[key], stream=f"ld{st['ld']}")

    def wload(src, nelem):
        s = st["ring"] % NRING
        st["ring"] += 1
        dst = ring[s][:, 0:nelem]
        P.op("pool", I.dma_start(out=dst, in_=src, max_dma_last_dim=4096),
             writes=[f"ring{s}"], stream=f"ring{s}")
        return ring[s], f"ring{s}"

    def fence(key):
        P.op("pool", I.memset(dummy[:, 0:1], 0.0), writes=[key])

    def next_mm():
        b = st["mm"] % 4
        st["mm"] += 1
        return b

    for c in range(KC):
        for (a, n, _) in FULL_BLOCKS[:1] + [(512, 1296, 1)]:
            pass
    for c in range(KC):
        st["ld"] += 1
        P.op("sp", (lambda c: I.dma_start(out=xT[:, c, :], in_=d_xT[:, c, :]))(c),
             writes=xk(c, 0, NT), stream=f"ld{st['ld']}")
        if c == 0:
            sp_load(condT, d_cond, "cond")
            sp_load(bmodT, d_bmod, "bmod")
            sp_load(lng, d_lng, "lng")
            sp_load(lnb, d_lnb, "lnb")
    sp_load(pscale, d_pscale, "pscale")
    sp_load(esink, d_sink, "esink")
    sp_load(kvalid, d_kvalid, "kvalid")
    sp_load(pinv.rearrange("p a b c -> p (a b c)"), d_pinv, "pinv")
    sp_load(pvalid, d_pvalid, "pvalid")
    P.op("pool", I.memset(onesm, 1.0 / 1024.0), writes=["onesm"])
    P.op("pool", I.memset(onesf, 1.0), writes=["onesf"])
    P.op("pool", I.memset(epsb, EPS2), writes=["epsb"])
    P.op("act", I.activation(out=scT, in_=condT, func=AF.Silu), reads=["cond"], writes=["scT"])

    ada_slots = [(dsb.bitcast(BF16), ["dsb0", "dsb1"], "ar0"),
                 (ar2, ["ar2"], "ar2"), (ar3, ["ar3"], "ar3")]
    ada_inflight = []

    def ada_issue(i, m, c):
        slot, key, stream = ada_slots[st["ada"] % len(ada_slots)]
        st["ada"] += 1
        oc = m * 8 + c
        P.op("pool", I.dma_start(out=slot[:, 0:KC * 128], in_=d_wmod[i, oc], max_dma_last_dim=4096),
             writes=list(key), stream=stream)
        ada_inflight.append((i, m, c, slot, key))

    def ada_compute():
        while ada_inflight:
            i, m, c, slot, key = ada_inflight.pop(0)
            col0 = (i * 9 + m) * 16
            w = slot[:, 0:KC * 128].rearrange("p (k n) -> p k n", k=KC)
            for kc in range(KC):
                P.op("pe", I.matmul(ps[7][:, col0 + 2 * c: col0 + 2 * c + 2], w[:, kc, :], scT[:, kc, :],
                                    start=(kc == 0), stop=(kc == KC - 1)),
                     reads=list(key) + ["scT"], writes=["ps7"])
            if c == KC - 1:
                src = ps[7][:, col0:col0 + 16].rearrange("p (c j) -> p c j", c=KC)
                bm = bmodT[:, i, m * 8:(m + 1) * 8].unsqueeze(2).broadcast_to([128, KC, 2])
                P.op("dve", I.tensor_tensor(out=modT[:, i, m * 8:(m + 1) * 8, :], in0=src, in1=bm, op=ALU.add),
                     reads=["ps7", "bmod"], writes=[f"mod{i}_{m}"])

    def emit_adaln(i, m):
        for c0 in range(0, KC, 4):
            for c in range(c0, c0 + 4):
                if len(ada_inflight) >= len(ada_slots):
                    ada_compute()
                ada_issue(i, m, c)
            ada_compute()

    def modv(i, m):
        return modT[:, i, m * 8:(m + 1) * 8, :]

    def emit_coefs(i, j):
        q = i * 3 + j
        gate = modv(i, 3 * j + 2)
        if i == 1 and j == 1:
            psb = pscale.unsqueeze(2).broadcast_to([128, KC, 2])
            P.op("dve", I.scalar_tensor_tensor(out=czT[:, q], in0=gate, scalar=1.0 / ALPHA, in1=psb,
                                                          op0=ALU.mult, op1=ALU.mult),
                 reads=[f"mod{i}_{3 * j + 2}", "pscale"], writes=[f"cz{q}"])
        else:
            fac = (1.0 if j == 1 else 0.5) / ALPHA
            P.op("dve", I.tensor_scalar(out=czT[:, q], in0=gate, scalar1=fac, scalar2=None, op0=ALU.mult),
                 reads=[f"mod{i}_{3 * j + 2}"], writes=[f"cz{q}"])
        if i == 1 and j == 2:
            return
        ni, nj = (i, j + 1) if j < 2 else (i + 1, 0)
        shift, scale = modv(ni, 3 * nj), modv(ni, 3 * nj + 1)
        rk = [f"mod{ni}_{3 * nj}", f"mod{ni}_{3 * nj + 1}", "lng", "lnb"]
        gb = lng[:, i, j, :].unsqueeze(2).broadcast_to([128, KC, 2])
        bb = lnb[:, i, j, :].unsqueeze(2).broadcast_to([128, KC, 2])
        P.op("dve", I.tensor_scalar(out=tmpc, in0=scale, scalar1=1.0, scalar2=None, op0=ALU.add),
             reads=rk, writes=["tmpc"])
        P.op("dve", I.tensor_tensor(out=GpT[:, q], in0=tmpc, in1=gb, op=ALU.mult),
             reads=["tmpc"] + rk, writes=[f"Gp{q}"])
        P.op("dve", I.tensor_tensor(out=BpT[:, q], in0=tmpc, in1=bb, op=ALU.mult),
             reads=["tmpc"] + rk, writes=[f"Bp{q}"])
        P.op("dve", I.tensor_tensor(out=BpT[:, q], in0=BpT[:, q], in1=shift, op=ALU.add),
             reads=[f"Bp{q}"] + rk, writes=[f"Bp{q}"])

    def drain(units):
        while units:
            units.pop(0)()

    def run_units(units, nleft):
        if not units:
            return
        k = -(-len(units) // max(nleft, 1))
        for _ in range(k):
            if units:
                units.pop(0)()

    def ffn(i, j, passes, extra):
        q = i * 3 + j
        jj = 0 if j == 0 else 1
        tasks = []
        for (kind, blocks, units) in passes:
            nch = FC if kind == "gu" else KC
            for oc in range(nch):
                tasks.append((kind, oc, blocks, units, nch - oc))
        handles = {}

        def issue(idx):
            kind, oc = tasks[idx][0], tasks[idx][1]
            if kind == "gu":
                handles[idx] = wload(d_wgu[i, jj, oc], 2 * KC * 128)
            else:
                handles[idx] = wload(d_wd[i, jj, oc], FC * 128)

        issue(0)
        issue(1)
        fence("BIG")
        for idx, (kind, oc, blocks, units, nleft) in enumerate(tasks):
            if idx + 2 < len(tasks):
                issue(idx + 2)
            slot, rkey = handles[idx]
            if kind == "gu":
                w = slot[:, 0:2 * KC * 128].rearrange("p (t k n) -> p t k n", t=2, k=KC)
                for (a, n, cond) in blocks:
                    ba = next_mm()
                    bb_ = next_mm()
                    for t, bank in ((0, ba), (1, bb_)):
                        for kc in range(KC):
                            P.op("pe", I.matmul(ps[bank][:, 0:n], w[:, t, kc, :], hT[:, kc, a:a + n],
                                                start=(kc == 0), stop=(kc == KC - 1)),
                                 reads=[rkey] + hk(kc, a, a + n), writes=[f"ps{bank}"])
                    s_ = st["sg"] % 2
                    st["sg"] += 1
                    P.op("act", I.activation(out=sg[s_][:, 0:n], in_=ps[ba][:, 0:n], func=AF.Silu),
                         reads=[f"ps{ba}"], writes=[f"sg{s_}"])
                    P.op("dve", I.tensor_tensor(out=actT[:, oc, a:a + n], in0=sg[s_][:, 0:n], in1=ps[bb_][:, 0:n], op=ALU.mult),
                         reads=[f"sg{s_}", f"ps{bb_}", "BIG"], writes=[f"a{oc}s{sg_}" for sg_ in segs(a, a + n)])
                extra()
            else:
                w = slot[:, 0:FC * 128].rearrange("p (k n) -> p k n", k=FC)
                for (a, n, cond) in blocks:
                    bank = next_mm()
                    for fc in range(FC):
                        P.op("pe", I.matmul(ps[bank][:, 0:n], w[:, fc, :], actT[:, fc, a:a + n],
                                            start=(fc == 0), stop=(fc == FC - 1)),
                             reads=[rkey, "BIG"] + [f"a{fc}s{s_}" for s_ in segs(a, a + n)], writes=[f"ps{bank}"])
                    evac_z(q, oc, bank, a, a, n, cond)
            run_units(units, nleft if kind == "d" else max(nleft - 6, 1))

    def evac_z(q, oc, bank, pa, xa, n, cond):
        P.op("dve", I.scalar_tensor_tensor(
            out=xT[:, oc, xa:xa + n], in0=ps[bank][:, pa - pa:n], scalar=czT[:, q, oc, cond:cond + 1],
            in1=xT[:, oc, xa:xa + n], op0=ALU.mult, op1=ALU.add),
            reads=[f"ps{bank}", f"cz{q}"] + xk(oc, xa, xa + n), writes=xk(oc, xa, xa + n))

    def layer_norm(i, j, blocks, hmode, as_units=False):
        q = i * 3 + j

        def stats(a, n, cond):
            for c in range(KC):
                s = st["zb"] % 2
                st["zb"] += 1
                P.op("act", I.activation(out=zb[s][:, 0:n], in_=xT[:, c, a:a + n], func=AF.Copy),
                     reads=xk(c, a, a + n), writes=[f"zb{s}"])
                P.op("act", I.activation(out=sqb[s][:, 0:n], in_=xT[:, c, a:a + n], func=AF.Square),
                     reads=xk(c, a, a + n), writes=[f"sqb{s}"])
                P.op("pe", I.matmul(ps[4][:, 0:n], onesm, zb[s][:, 0:n], start=(c == 0), stop=(c == KC - 1)),
                     reads=["onesm", f"zb{s}"], writes=["ps4"])
                P.op("pe", I.matmul(ps[5][:, 0:n], onesm, sqb[s][:, 0:n], start=(c == 0), stop=(c == KC - 1)),
                     reads=["onesm", f"sqb{s}"], writes=["ps5"])

        def finalize(a, n, cond):
            P.op("act", I.activation(out=lnm[:, 0:n], in_=ps[4][:, 0:n], func=AF.Copy),
                 reads=["ps4"], writes=["lnm"])
            P.op("act", I.activation(out=lnv[:, 0:n], in_=ps[4][:, 0:n], func=AF.Square),
                 reads=["ps4"], writes=["lnv"])
            P.op("dve", I.scalar_tensor_tensor(out=lnv[:, 0:n], in0=lnv[:, 0:n], scalar=-1.0, in1=ps[5][:, 0:n],
                                               op0=ALU.mult, op1=ALU.add),
                 reads=["ps5", "lnv"], writes=["lnv"])
            P.op("act", I.activation(out=lnv[:, 0:n], in_=lnv[:, 0:n], func=AF.Ln, bias=epsb, scale=1.0),
                 reads=["lnv", "epsb"], writes=["lnv"])
            P.op("act", I.activation(out=lnv[:, 0:n], in_=lnv[:, 0:n], func=AF.Exp, scale=-0.5),
                 reads=["lnv"], writes=["lnv"])

        def apply(a, n, cond):
            for c in range(KC):
                xs = xT[:, c, a:a + n]
                keys = xk(c, a, a + n)
                P.op("dve", I.tensor_tensor(out=xs, in0=xs, in1=lnm[:, 0:n], op=ALU.subtract),
                     reads=keys + ["lnm"], writes=keys)
            for c in range(KC):
                xs = xT[:, c, a:a + n]
                keys = xk(c, a, a + n)
                P.op("dve", I.tensor_tensor(out=xs, in0=xs, in1=lnv[:, 0:n], op=ALU.mult),
                     reads=keys + ["lnv"], writes=keys)
            for c in range(KC):
                xs = xT[:, c, a:a + n]
                keys = xk(c, a, a + n)
                if hmode == "h":
                    P.op("act", I.activation(
                        out=hT[:, c, a:a + n], in_=xs, func=AF.Identity,
                        bias=BpT[:, q, c, cond:cond + 1], scale=GpT[:, q, c, cond:cond + 1]),
                        reads=keys + [f"Gp{q}", f"Bp{q}"], writes=hk(c, a, a + n))
                elif hmode == "hp":
                    pieces = [(a, n)] if a >= 512 else [(0, 256), (256, 256)]
                    for (pa, pn) in pieces:
                        hp0 = hpcol(pa)
                        P.op("act", I.activation(
                            out=hpT[:, c, hp0:hp0 + pn], in_=xT[:, c, pa:pa + pn], func=AF.Identity,
                            bias=BpT[:, q, c, cond:cond + 1], scale=GpT[:, q, c, cond:cond + 1]),
                            reads=keys + [f"Gp{q}", f"Bp{q}", "BIG"], writes=[f"hp{c}"])
            for c in range(KC):
                xs = xT[:, c, a:a + n]
                keys = xk(c, a, a + n)
                if True:
                    P.op("dve", I.tensor_scalar(
                        out=xs, in0=xs, scalar1=lng[:, i, j, c:c + 1], scalar2=lnb[:, i, j, c:c + 1],
                        op0=ALU.mult, op1=ALU.add),
                        reads=keys + ["lng", "lnb"], writes=keys)
                else:
                    P.op("act", I.activation(
                        out=xs, in_=xs, func=AF.Identity,
                        bias=lnb[:, i, j, c:c + 1], scale=lng[:, i, j, c:c + 1]),
                        reads=keys + ["lng", "lnb"], writes=keys)

        if as_units:
            units = []
            for blk in blocks:
                units.append(lambda blk=blk: stats(*blk))
                units.append(lambda blk=blk: finalize(*blk))
                units.append(lambda blk=blk: apply(*blk))
            return units
        stats(*blocks[0])
        for bi, blk in enumerate(blocks):
            finalize(*blk)
            if bi + 1 < len(blocks):
                stats(*blocks[bi + 1])
            apply(*blk)
        return []

    o = 0

    def carve(nelem_bf16):
        nonlocal o
        v = BIG[:, o:o + nelem_bf16]
        o += nelem_bf16
        return v
    QW = 512 + MN
    qT = carve(KC * QW).rearrange("p (c n) -> p c n", c=KC)
    kT = carve(4 * NT).rearrange("p (g n) -> p g n", g=4)
    va0 = carve(17 * 4 * 68).rearrange("p (k g n) -> p k g n", k=17, g=4)
    va1 = carve(17 * 4 * 128).rearrange("p (k g n) -> p k g n", k=17, g=4)
    kcT = carve(4 * 256).rearrange("p (g n) -> p g n", g=4)
    cosT = carve(2 * SFN).bitcast(F32)
    sinT = carve(2 * SFN).bitcast(F32)
    assert o <= FC * NT, o
    o = 0
    hpT = carve(2 * KC * HPW).bitcast(F32).rearrange("p (c n) -> p c n", c=KC)
    Sa = carve(2 * 2 * HPW).bitcast(F32).rearrange("p (c n) -> p c n", c=2)
    Sb = carve(2 * 2 * HPW).bitcast(F32).rearrange("p (c n) -> p c n", c=2)
    assert o <= FC * NT, o

    def attention(wo_units=None, pre_units=None):
        q = 1
        fence("BIG")
        B = ["BIG"]
        sp_load(cosT, d_cos, "cosT", reads=B)
        sp_load(sinT, d_sin, "sinT", reads=B)
        P.op("pool", I.dma_start(out=kcT.rearrange("p g n -> p (g n)"), in_=d_ck, max_dma_last_dim=4096),
             reads=B, writes=["kcT"], stream="kcT")
        P.op("pool", I.dma_start(out=masks.rearrange("p a b -> p (a b)"), in_=d_masks, max_dma_last_dim=4096),
             writes=["masks"], stream="masks")
        P.op("act", I.activation(out=esink, in_=esink, func=AF.Exp), reads=["esink"], writes=["esink"])
        P.op("pool", I.memset(va0.rearrange("p k g n -> p (k g n)"), 0.0), reads=B, writes=["va0", "cosT", "sinT"][:1])
        P.op("pool", I.memset(va1.rearrange("p k g n -> p (k g n)"), 0.0), reads=B, writes=["va1"])
        for kci in range(17):
            if 4 <= kci < 15:
                src = kvalid[:, kci - 4:kci - 3].unsqueeze(1).broadcast_to([128, 4, 1])
                rk = ["kvalid"]
            else:
                src = onesf[:, 0:1].unsqueeze(1).broadcast_to([128, 4, 1])
                rk = ["onesf"]
            P.op("dve", (lambda kci, src: I.tensor_copy(out=va0[:, kci, :, 64:65], in_=src))(kci, src),
                 reads=rk + B, writes=["va0"])
            P.op("dve", (lambda kci, src: I.tensor_copy(out=va1[:, kci, :, 0:1], in_=src))(kci, src),
                 reads=rk + B, writes=["va1"])
        P.op("pool", I.dma_start(out=cvb.rearrange("p j g d -> p (j g d)"), in_=d_cv.rearrange("p j g d -> p (j g d)"),
                                 max_dma_last_dim=4096), writes=["cvb"], stream="cvb")
        for jx in range(2):
            P.op("act", (lambda jx: I.activation(out=va0[:, 15 + jx, :, 0:64], in_=cvb[:, jx], func=AF.Copy))(jx),
                 reads=["cvb"] + B, writes=["va0"])
            P.op("act", (lambda jx: I.activation(out=va1[:, 15 + jx, :, 64:128], in_=cvb[:, jx], func=AF.Copy))(jx),
                 reads=["cvb"] + B, writes=["va1"])

        if ATT_CUT <= 1:
            return
        ptasks = []

        def proj(oc_list, blocks, evac):
            ptasks.append((oc_list, blocks, evac))

        def run_ptasks(units):
            loaded = {}

            def load(t):
                if t < len(ptasks) and t not in loaded:
                    loaded[t] = [wload(d_wqk[oc], KC * 128) for oc in ptasks[t][0]]
            for t, (oc_list, blocks, evac) in enumerate(ptasks):
                load(t)
                if t + 1 < len(ptasks) and len(oc_list) + len(ptasks[t + 1][0]) <= NRING:
                    load(t + 1)
                slots = [(sl[:, 0:KC * 128].rearrange("p (k n) -> p k n", k=KC), rk) for (sl, rk) in loaded[t]]
                for (a, n, cond) in blocks:
                    banks = []
                    for (w, rkey) in slots:
                        bank = next_mm()
                        banks.append(bank)
                        for kc in range(KC):
                            P.op("pe", I.matmul(ps[bank][:, 0:n], w[:, kc, :], hT[:, kc, a:a + n],
                                                start=(kc == 0), stop=(kc == KC - 1)),
                                 reads=[rkey] + hk(kc, a, a + n), writes=[f"ps{bank}"])
                    evac(banks, a, n)
                if t < 12:
                    run_units(units, 12 - t)

        PB = [(0, 512, 0)]
        SFB = [(512, 432, 1), (944, 432, 1), (1376, 432, 1)]
        SMB = [(640, 352, 1), (992, 352, 1), (1344, 336, 1)]
        for c in range(KC):
            def ev(banks, a, n, c=c):
                P.op("act", I.activation(out=qT[:, c, a:a + n], in_=ps[banks[0]][:, 0:n], func=AF.Copy),
                     reads=[f"ps{banks[0]}"] + B, writes=[f"qT{c}"])
            proj([c], PB, ev)
        for g in range(4):
            def ev(banks, a, n, g=g):
                P.op("act", I.activation(out=kT[:, g, a:a + n], in_=ps[banks[0]][:, 0:n], func=AF.Copy),
                     reads=[f"ps{banks[0]}"] + B, writes=[f"kT{g}"])
                buf, bk = (sg[g % 2], f"sg{g % 2}")
                P.op("dve", I.tensor_copy(out=buf[:, 0:n], in_=ps[banks[0]][:, 0:n]),
                     reads=[f"ps{banks[0]}", f"kT{g}"], writes=[bk])
                P.op("sp", I.dma_start(out=d_sk[:, g, :], in_=buf[0:64, 0:n]),
                     reads=[bk], stream=f"sk{g}", final=True)
            proj([16 + g], PB, ev)

        def rope_ev(dst_fn, wkeys, toff):
            def ev(banks, a, n):
                ca = a - SF0
                P.op("dve", I.tensor_tensor(out=lnm[:, 0:n], in0=ps[banks[0]][:, 0:n], in1=cosT[:, ca:ca + n], op=ALU.mult),
                     reads=[f"ps{banks[0]}", "cosT"] + B, writes=["lnm"])
                P.op("dve", I.tensor_tensor(out=lnv[:, 0:n], in0=ps[banks[1]][:, 0:n], in1=sinT[:, ca:ca + n], op=ALU.mult),
                     reads=[f"ps{banks[1]}", "sinT"] + B, writes=["lnv"])
                P.op("dve", I.tensor_tensor(out=dst_fn(a, n), in0=lnm[:, 0:n], in1=lnv[:, 0:n], op=ALU.add),
                     reads=["lnm", "lnv"] + B, writes=wkeys)
            return ev
        for c in range(KC):
            proj([c, 8 + c], SMB, rope_ev(lambda a, n, c=c: qT[:, c, 512 + a - M0: 512 + a - M0 + n], [f"qT{c}"], 0))
        for g in range(4):
            proj([16 + g, 20 + g], SFB, rope_ev(lambda a, n, g=g: kT[:, g, a:a + n], [f"kT{g}"], 0))

        run_ptasks(pre_units if pre_units is not None else [])
        drain(pre_units if pre_units is not None else [])
        slot, rkey = wload(d_wv, KC * 256)
        wv = slot[:, 0:KC * 256].rearrange("p (k n) -> p k n", k=KC)
        for kci in range(15):
            a = kci * 128
            nk = 128 if kci < 14 else 16
            bank = next_mm()
            for kc in range(KC):
                P.op("pe", (lambda bank, kc, a, nk: I.matmul(
                    ps[bank][0:nk, 0:256], hT[:, kc, a:a + nk], wv[:, kc, :],
                    start=(kc == 0), stop=(kc == KC - 1)))(bank, kc, a, nk),
                    reads=[rkey] + hk(kc, a, a + nk), writes=[f"ps{bank}"])
            src = ps[bank][0:nk, 0:256].rearrange("p (g d) -> p g d", g=4)
            if kci < 4:
                P.op("act", (lambda kci, src, nk: I.activation(out=va0[0:nk, kci, :, 0:64], in_=src, func=AF.Copy))(kci, src, nk),
                     reads=[f"ps{bank}"] + B, writes=["va0"])
                P.op("act", (lambda kci, src, nk: I.activation(out=va1[0:nk, kci, :, 64:128], in_=src, func=AF.Copy))(kci, src, nk),
                     reads=[f"ps{bank}"] + B, writes=["va1"])
                buf, bk = (sg[kci % 2], f"sg{kci % 2}")
                P.op("dve", (lambda buf, bank: I.tensor_copy(out=buf[:, 0:256], in_=ps[bank][:, 0:256]))(buf, bank),
                     reads=[f"ps{bank}", "va1"], writes=[bk])
                P.op("sp", (lambda buf, kci: I.dma_start(out=d_sv[:, kci, :], in_=buf[:, 0:256]))(buf, kci),
                     reads=[bk], stream=f"sv{kci}", final=True)
            else:
                kv = kvalid[0:nk, kci - 4:kci - 3]
                P.op("act", (lambda kci, src, nk, kv: I.activation(out=va0[0:nk, kci, :, 0:64], in_=src, func=AF.Copy, scale=kv))(kci, src, nk, kv),
                     reads=[f"ps{bank}", "kvalid"] + B, writes=["va0"])
                P.op("act", (lambda kci, src, nk, kv: I.activation(out=va1[0:nk, kci, :, 64:128], in_=src, func=AF.Copy, scale=kv))(kci, src, nk, kv),
                     reads=[f"ps{bank}", "kvalid"] + B, writes=["va1"])

        if ATT_CUT <= 3:
            return
        pairs = []
        for b in range(2):
            for qt in range(2):
                kch = [(2 * b + jx, (lambda g, jx=jx, b=b: kT[:, g, b * 256 + jx * 128: b * 256 + jx * 128 + 128]), 128, None)
                       for jx in range(2)]
                for g in range(4):
                    pairs.append((b * 256 + qt * 128, 128, kch, b * 256 + qt * 128, g))
        if ATT_CUT > 4:
            for ti in range(9):
                nq = 128 if ti < 8 else 16
                kch = []
                for r in range(3):
                    ci = ti + r
                    nk = 128 if ci < 10 else 16
                    mid = 0 if r == 0 else (1 if r == 2 else None)
                    kch.append((4 + ci, (lambda g, ci=ci, nk=nk: kT[:, g, SF0 + ci * 128: SF0 + ci * 128 + nk]), nk, mid))
                for jx in range(2):
                    kch.append((15 + jx, (lambda g, jx=jx: kcT[:, g, jx * 128:(jx + 1) * 128]), 128, None))
                for g in range(4):
                    pairs.append((512 + ti * 128, nq, kch, M0 + ti * 128, g))

        def make_tail(nq, ocol, g, po, po1):
            d0 = dsb[64:65, 0:2 * nq]
            d1 = dsb[0:1, 2 * nq:4 * nq]
            stt = {}

            def part1():
                es0 = esink[64:65, 4 * g:4 * g + 2].unsqueeze(2).broadcast_to([1, 2, nq])
                es1 = esink[0:1, 4 * g + 2:4 * g + 4].unsqueeze(2).broadcast_to([1, 2, nq])
                P.op("dve", I.tensor_tensor(out=d0.rearrange("p (c n) -> p c n", c=2),
                                            in0=ps[po][64:65, 0:2 * nq].rearrange("p (c n) -> p c n", c=2),
                                            in1=es0, op=ALU.add),
                     reads=[f"ps{po}", "esink"], writes=["dsb0"])
                P.op("dve", I.tensor_tensor(out=d1.rearrange("p (c n) -> p c n", c=2),
                                            in0=ps[po1][0:1, 2 * nq:4 * nq].rearrange("p (c n) -> p c n", c=2),
                                            in1=es1, op=ALU.add),
                     reads=[f"ps{po1}", "esink"], writes=["dsb1"])
                P.op("act", I.activation(out=d0, in_=d0, func=AF.Ln), reads=["dsb0"], writes=["dsb0"])
                P.op("act", I.activation(out=d1, in_=d1, func=AF.Ln), reads=["dsb1"], writes=["dsb1"])
                P.op("act", I.activation(out=d0, in_=d0, func=AF.Exp, scale=-1.0), reads=["dsb0"], writes=["dsb0"])
                P.op("act", I.activation(out=d1, in_=d1, func=AF.Exp, scale=-1.0), reads=["dsb1"], writes=["dsb1"])
                bA, bB = po1, po
                stt["b"] = (bA, bB)
                P.op("pe", I.matmul(ps[bA][:, 0:2 * nq], onesf[64:65, 0:128], d0, start=True, stop=True),
                     reads=["dsb0", "onesf"], writes=[f"ps{bA}"])
                P.op("pe", I.matmul(ps[bB][:, 2 * nq:4 * nq], onesf[0:1, 0:128], d1, start=True, stop=True),
                     reads=["dsb1", "onesf"], writes=[f"ps{bB}"])

            def part2():
                bA, bB = stt["b"]
                s2 = st["sg"] % 2
                st["sg"] += 1
                P.op("dve", I.tensor_copy(out=sg[s2][:, 0:2 * nq], in_=ps[bA][:, 0:2 * nq]),
                     reads=[f"ps{bA}"], writes=[f"sg{s2}"])
                P.op("dve", I.tensor_copy(out=sg[s2][:, 2 * nq:4 * nq], in_=ps[bB][:, 2 * nq:4 * nq]),
                     reads=[f"ps{bB}"], writes=[f"sg{s2}"])
                ok = hk(2 * g, ocol, ocol + nq) + hk(2 * g + 1, ocol, ocol + nq)
                P.op("dve", I.tensor_tensor(
                    out=hT[0:64, 2 * g:2 * g + 2, ocol:ocol + nq],
                    in0=ps[po][0:64, 0:2 * nq].rearrange("p (c n) -> p c n", c=2),
                    in1=sg[s2][0:64, 0:2 * nq].rearrange("p (c n) -> p c n", c=2), op=ALU.mult),
                    reads=[f"ps{po}", f"sg{s2}"], writes=ok)
                P.op("dve", I.tensor_tensor(
                    out=hT[64:128, 2 * g:2 * g + 2, ocol:ocol + nq],
                    in0=ps[po1][64:128, 2 * nq:4 * nq].rearrange("p (c n) -> p c n", c=2),
                    in1=sg[s2][64:128, 2 * nq:4 * nq].rearrange("p (c n) -> p c n", c=2), op=ALU.mult),
                    reads=[f"ps{po1}", f"sg{s2}"], writes=ok)
            return part1, part2

        prev_tail = None
        for pi, (qc0, nq, kchunks, ocol, g) in enumerate(pairs):
            po, po1 = (4, 5) if pi % 2 == 0 else (6, 7)
            nkc = len(kchunks)
            slots = {}
            sbanks = {}

            def score(ki):
                kci, ksrc, nk, mid = kchunks[ki]
                kg = ksrc(g)
                slots[ki] = st["pT"] % 3
                st["pT"] += 1
                bl = []
                for half in range(2):
                    bank = next_mm()
                    bl.append(bank)
                    p0 = half * 64
                    P.op("pe", I.matmul(
                        ps[bank][0:nk, 0:2 * nq].rearrange("p (c n) -> p c n", c=2),
                        kg[p0:p0 + 64, 0:nk], qT[p0:p0 + 64, 2 * g:2 * g + 2, qc0:qc0 + nq],
                        start=True, stop=True),
                        reads=[f"kT{g}", "kcT", f"qT{2 * g}", f"qT{2 * g + 1}"] + B, writes=[f"ps{bank}"])
                sbanks[ki] = bl

            def exp_pv(ki, between=None):
                kci, ksrc, nk, mid = kchunks[ki]
                s = slots[ki]
                for half in range(2):
                    bank = sbanks[ki][half]
                    P.op("act", I.activation(
                        out=pT[s][0:nk, half * 2 * nq:(half + 1) * 2 * nq], in_=ps[bank][0:nk, 0:2 * nq],
                        func=AF.Exp, scale=ATT_SCALE),
                        reads=[f"ps{bank}"], writes=[f"pT{s}h{half}"])
                if mid is not None:
                    pv = pT[s][0:nk, 0:4 * nq].rearrange("p (c n) -> p c n", c=4)
                    mk = masks[0:nk, mid, 0:nq].unsqueeze(1).broadcast_to([nk, 4, nq])
                    P.op("dve", I.tensor_tensor(out=pv, in0=pv, in1=mk, op=ALU.mult),
                         reads=[f"pT{s}h0", f"pT{s}h1", "masks"], writes=[f"pT{s}h0", f"pT{s}h1"])
                if between is not None:
                    between()
                P.op("pe", I.matmul(
                    ps[po][0:65, 0:2 * nq], va0[0:nk, kci, g, 0:65], pT[s][0:nk, 0:2 * nq],
                    start=(ki == 0), stop=(ki == nkc - 1)),
                    reads=[f"pT{s}h0", "va0"] + B, writes=[f"ps{po}"])
                P.op("pe", I.matmul(
                    ps[po1][:, 2 * nq:4 * nq], va1[0:nk, kci, g, :], pT[s][0:nk, 2 * nq:4 * nq],
                    start=(ki == 0), stop=(ki == nkc - 1)),
                    reads=[f"pT{s}h1", "va1"] + B, writes=[f"ps{po1}"])

            score(0)
            if nkc > 1:
                score(1)
            done1 = done2 = False
            for ki in range(nkc):
                exp_pv(ki, (lambda ki=ki: score(ki + 2)) if ki + 2 < nkc else None)
                if prev_tail is not None:
                    if ki == min(1, nkc - 1) and not done1:
                        prev_tail[0]()
                        done1 = True
                    if ki == min(3, nkc - 1) and done1 and not done2 and ki >= 1:
                        prev_tail[1]()
                        done2 = True
            if prev_tail is not None and not done2:
                if not done1:
                    prev_tail[0]()
                prev_tail[1]()
            prev_tail = make_tail(nq, ocol, g, po, po1)
        prev_tail[0]()
        prev_tail[1]()

        if ATT_CUT <= 5:
            return
        wo_units = wo_units if wo_units is not None else []
        for pi_, pblocks in enumerate((MAIN_BLOCKS[:2], MAIN_BLOCKS[2:])):
            for oc in range(KC):
                slot, rkey = wload(d_wo[oc], KC * 128)
                w = slot[:, 0:KC * 128].rearrange("p (k n) -> p k n", k=KC)
                for (a, n, cond) in pblocks:
                    bank = next_mm()
                    for kc in range(KC):
                        P.op("pe", I.matmul(ps[bank][:, 0:n], w[:, kc, :], hT[:, kc, a:a + n],
                                            start=(kc == 0), stop=(kc == KC - 1)),
                             reads=[rkey] + hk(kc, a, a + n), writes=[f"ps{bank}"])
                    evac_z(q, oc, bank, a, a, n, cond)
                if pi_ == 1:
                    run_units(wo_units, KC - oc)

    def pool_prepare():
        fence("BIG")
        P.op("pool", I.memset(hpT.rearrange("p c n -> p (c n)"), 0.0), reads=["BIG"],
             writes=[f"hp{c}" for c in range(KC)])

    def pool_mix():
        q = 4
        B = ["BIG"]
        allhp = [f"hp{c}" for c in range(KC)]
        for (col, pv0) in ((536, 0), (1568, 8)):
            P.op("dve", (lambda col, pv0: I.tensor_tensor(
                out=hpT[:, :, col:col + 8], in0=hpT[:, :, col:col + 8],
                in1=pvalid[:, pv0:pv0 + 8].unsqueeze(1).broadcast_to([128, KC, 8]), op=ALU.mult))(col, pv0),
                reads=allhp + ["pvalid"] + B, writes=allhp)
        W = HPW
        edges = []
        for (hp, xa, nn) in HP_SEGS:
            if nn == 256:
                edges.append((hp, xa))
                edges.append((hp + nn - 8, xa + nn - 8))
            else:
                edges.append((hp + 8, xa + 8))
                edges.append((hp + nn - 16, xa + nn - 16))
        for gi in range(4):
            eng = "dve"
            hsl = hpT[:, 2 * gi:2 * gi + 2, :]
            hkeys = [f"hp{2 * gi}", f"hp{2 * gi + 1}"]
            bufs = [(Sa, "Sa"), (Sb, "Sb")]
            cur, ck = hsl, hkeys
            steps = [(1, 0, 1), (1, 1, 2), (2, 2, 4), (4, 4, 8)][:gi + 1]
            lo, hi = 0, W
            for si, (ls, rs, _) in enumerate(steps):
                dst, dk = bufs[si % 2]
                nlo, nhi = lo + ls, hi - rs
                P.op(eng, (lambda dst, cur, nlo, nhi, ls, rs: I.tensor_tensor(
                    out=dst[:, :, nlo:nhi], in0=cur[:, :, nlo - ls:nhi - ls], in1=cur[:, :, nlo + rs:nhi + rs], op=ALU.add))(dst, cur, nlo, nhi, ls, rs),
                    reads=ck + B, writes=[dk])
                cur, ck = dst, [dk]
                lo, hi = nlo, nhi
            w = POOL_W[gi]
            for (hp, xa, nn) in HP_SEGS:
                okeys = hk(2 * gi, xa, xa + nn) + hk(2 * gi + 1, xa, xa + nn)
                P.op("dve", (lambda cur, hp, xa, nn: I.scalar_tensor_tensor(
                    out=hT[:, 2 * gi:2 * gi + 2, xa:xa + nn], in0=cur[:, :, hp:hp + nn], scalar=1.0 / w,
                    in1=hsl[:, :, hp:hp + nn], op0=ALU.mult, op1=ALU.subtract))(cur, hp, xa, nn),
                    reads=ck + hkeys + B, writes=okeys)
            for ei, (hp, xa) in enumerate(edges):
                okeys = hk(2 * gi, xa, xa + 8) + hk(2 * gi + 1, xa, xa + 8)
                pv = pinv[:, gi, ei, :].unsqueeze(1).broadcast_to([128, 2, 8])
                et = etmp[eng]
                P.op(eng, (lambda cur, hp, pv, et: I.tensor_tensor(
                    out=et, in0=cur[:, :, hp:hp + 8], in1=pv, op=ALU.mult))(cur, hp, pv, et),
                    reads=ck + ["pinv"] + B, writes=["etmp" + eng])
                P.op(eng, (lambda hp, xa, et: I.tensor_tensor(
                    out=hT[:, 2 * gi:2 * gi + 2, xa:xa + 8], in0=et,
                    in1=hsl[:, :, hp:hp + 8], op=ALU.subtract))(hp, xa, et),
                    reads=["etmp" + eng] + hkeys + B, writes=okeys)
        for oc in range(KC):
            gi = oc // 2
            slot, rkey = wload(d_poolw[oc], 2 * 128)
            w2 = slot[:, 0:256].rearrange("p (k n) -> p k n", k=2)
            for (a, n, cond) in MAIN_BLOCKS:
                bank = next_mm()
                for kk in range(2):
                    P.op("pe", (lambda bank, kk, a, n, w2, gi: I.matmul(
                        ps[bank][:, 0:n], w2[:, kk, :], hT[:, 2 * gi + kk, a:a + n],
                        start=(kk == 0), stop=(kk == 1)))(bank, kk, a, n, w2, gi),
                        reads=[rkey] + hk(2 * gi + kk, a, a + n), writes=[f"ps{bank}"])
                evac_z(q, oc, bank, a, a, n, cond)

    pending = []

    def extra(npieces=3):
        ada_compute()
        for _ in range(npieces):
            if pending:
                ada_issue(*pending.pop(0))

    def flush():
        ada_compute()
        while pending:
            for _ in range(len(ada_slots)):
                if pending:
                    ada_issue(*pending.pop(0))
            ada_compute()

    emit_adaln(0, 0)
    emit_adaln(0, 1)
    P.op("dve", I.tensor_scalar(out=S0T, in0=modv(0, 1), scalar1=1.0, scalar2=None, op0=ALU.add),
         reads=["mod0_1"], writes=["S0T"])
    for (a, n, cond) in FULL_BLOCKS:
        for c in range(KC):
            P.op("act", (lambda a, n, cond, c: I.activation(
                out=hT[:, c, a:a + n], in_=xT[:, c, a:a + n], func=AF.Identity,
                bias=modT[:, 0, 0 * 8 + c, cond:cond + 1], scale=S0T[:, c, cond:cond + 1]))(a, n, cond, c),
                reads=xk(c, a, a + n) + ["S0T", "mod0_0"], writes=hk(c, a, a + n))
    pending += [(0, m, c) for m in range(2, 9) for c in range(KC)]
    stage = 0

    def done():
        nonlocal stage
        stage += 1
        return stage >= upto

    def finish():
        for c in range(KC):
            P.op("sp", I.dma_start(out=d_y[:, c, 0:512], in_=xT[:, c, 0:512]),
                 reads=xk(c, 0, 512), stream=f"y{c}a", final=True)
        for (lo, hi) in ((OWN0, 992), (992, 1344), (1344, OWN0 + OWNN)):
            for c in range(KC):
                P.op("sp", I.dma_start(out=d_y[:, c, 512 + lo - OWN0:512 + hi - OWN0], in_=xT[:, c, lo:hi]),
                     reads=xk(c, lo, hi), stream=f"y{c}b{lo}", final=True)

    def run():
        FA, FB = FULL_BLOCKS[:2], FULL_BLOCKS[2:]
        MA, MB = MAIN_BLOCKS[:2], MAIN_BLOCKS[2:]
        uA = layer_norm(0, 0, FA, "h", as_units=True)
        ffn(0, 0, [("gu", FULL_BLOCKS, []), ("d", FA, []), ("d", FB, uA)], extra_with_coefs(0, 0, FC))
        drain(uA)
        uB0 = layer_norm(0, 0, FB, "h", as_units=True)
        emit_coefs(0, 1)
        pending.extend([(1, m, c) for m in range(9) for c in range(KC)])
        uA = layer_norm(0, 1, MA, "h", as_units=True)
        attention(uA, uB0)
        drain(uA)
        uB = layer_norm(0, 1, MB, "h", as_units=True)
        uA = layer_norm(0, 2, MA, "h", as_units=True)
        ffn(0, 2, [("gu", MA, uB), ("gu", MB, []), ("d", MA, []), ("d", MB, uA)], extra_with_coefs(0, 2, 2 * FC))
        drain(uA)
        uB = layer_norm(0, 2, MB, "h", as_units=True)
        emit_coefs(1, 0)
        ffn(1, 0, [("gu", MA, uB), ("gu", MB, []), ("d", MA, []), ("d", MB, [])], lambda: None)
        emit_coefs(1, 1)
        pool_prepare()
        layer_norm(1, 0, MAIN_BLOCKS, "hp")
        pool_mix()
        emit_coefs(1, 2)
        layer_norm(1, 1, MA, "h")
        uB = layer_norm(1, 1, MB, "h", as_units=True)
        uA = layer_norm(1, 2, MA, None, as_units=True)
        uB1 = layer_norm(1, 2, MB[:1], None, as_units=True)
        ffn(1, 2, [("gu", MA, uB), ("gu", MB, []), ("d", MA, []), ("d", MB[:1], uA), ("d", MB[1:], uB1)], lambda: None)
        drain(uA)
        drain(uB1)
        layer_norm(1, 2, MB[1:], None)

    def extra_with_coefs(i, j, total):
        state = {"n": 0}

        def f():
            state["n"] += 1
            extra(3 if (i, j) == (0, 0) else 2)
            if state["n"] == total:
                flush()
                emit_coefs(i, j)
        return f

    run()
    finish()

    with ExitStack() as es:
        sems = {n: es.enter_context(nc.semaphore(n)) for n in P.sem_names()}
        with nc.Block() as block:
            P.emit(block, sems)
    return nc


def _fm(x2d):
    T = x2d.shape[0]
    return np.ascontiguousarray(x2d.reshape(T, KC, 128).transpose(2, 1, 0))


def _wchunks(w, kc, noc):
    return np.ascontiguousarray(w.reshape(kc, 128, noc, 128).transpose(2, 1, 0, 3).reshape(noc, 128, kc * 128))


def prepare_inputs(x_prompt, x_sample, cache_k, cache_v, c, c_ctx, w_mod, b_mod, ln_g, ln_b,
                   ffn_w_gate, ffn_w_up, ffn_w_down, attn_w_qkv, attn_w_o, attn_sink, pool_w, pool_scale):
    f32 = np.float32
    shared = {}
    shared["wmod"] = np.stack([_wchunks(w_mod[i], KC, 72) for i in range(2)])
    shared["bmodT"] = np.ascontiguousarray(b_mod.reshape(2, 72, 128).transpose(2, 0, 1))
    shared["lngT"] = np.ascontiguousarray(ln_g.reshape(2, 3, KC, 128).transpose(3, 0, 1, 2))
    shared["lnbT"] = np.ascontiguousarray(ln_b.reshape(2, 3, KC, 128).transpose(3, 0, 1, 2))
    wgu = np.empty((2, 2, FC, 128, 2, KC * 128), f32)
    wd = np.empty((2, 2, KC, 128, FC * 128), f32)
    for i in range(2):
        for j in range(2):
            wgu[i, j, :, :, 0, :] = _wchunks(ffn_w_gate[i, j], KC, FC)
            wgu[i, j, :, :, 1, :] = _wchunks(ffn_w_up[i, j], KC, FC)
            wd[i, j] = _wchunks(ffn_w_down[i, j], FC, KC)
    shared["wgu"] = wgu.reshape(2, 2, FC, 128, 2 * KC * 128)
    shared["wd"] = wd
    wqkv = attn_w_qkv[0]
    perm = np.concatenate([np.arange(16, 32), np.arange(0, 16), np.arange(48, 64), np.arange(32, 48)])
    wq = wqkv[:, :1024]
    wk = wqkv[:, 1024:1280]
    wvv = wqkv[:, 1280:1536]
    wqp = wq.reshape(1024, 16, 64)[:, :, perm].reshape(1024, 1024)
    wkd = np.repeat(wk.reshape(1024, 4, 1, 64), 2, axis=2).reshape(1024, 512)
    wkp = wk.reshape(1024, 4, 64)[:, :, perm]
    wkpd = np.repeat(wkp.reshape(1024, 4, 1, 64), 2, axis=2).reshape(1024, 512)
    wall = np.concatenate([wq, wqp, wkd, wkpd], axis=1)
    shared["wqk"] = _wchunks(wall, KC, 24)
    shared["wv"] = np.ascontiguousarray(wvv.reshape(KC, 128, 256).transpose(1, 0, 2).reshape(128, KC * 256))
    shared["wo"] = _wchunks(attn_w_o[0], KC, KC)
    pw = np.empty((KC, 128, 2 * 128), f32)
    for oc in range(KC):
        gi = oc // 2
        blk = pool_w[0, gi][:, (oc % 2) * 128:(oc % 2) * 128 + 128]
        pw[oc] = blk.reshape(2, 128, 128).transpose(1, 0, 2).reshape(128, 256)
    shared["poolw"] = pw
    shared["pscaleT"] = np.ascontiguousarray(pool_scale[0].reshape(KC, 128).T)
    sk_ord = attn_sink[0].reshape(4, 2, 2).transpose(0, 2, 1).reshape(16)
    shared["sinkb"] = np.ascontiguousarray(np.broadcast_to(sk_ord[None, :], (128, 16)))
    ql = np.arange(128)
    mk = np.empty((128, 2, 128), f32)
    mk[:, 0, :] = (ql[:, None] >= ql[None, :])
    mk[:, 1, :] = (ql[:, None] <= ql[None, :])
    shared["masks"] = mk.reshape(128, 256)

    inv = np.power(10000.0, -np.arange(16, dtype=np.float64) / 16.0)
    sign = np.concatenate([-np.ones(16), np.ones(16), -np.ones(16), np.ones(16)]).astype(f32)
    in_maps = []
    for core in range(8):
        sbi, qd = core // 4, core % 4
        s0 = qd * 1024
        m = dict(shared)
        xcols = np.zeros((NT, D), f32)
        xcols[0:256] = x_prompt[2 * core]
        xcols[256:512] = x_prompt[2 * core + 1]
        pos = s0 - HALO + np.arange(SFN)
        ok = (pos >= 0) & (pos < 4096)
        xcols[512:][ok] = x_sample[sbi][pos[ok]]
        m["xT"] = _fm(xcols)
        cond = np.stack([c_ctx, c[sbi]], axis=1)
        m["condT"] = np.ascontiguousarray(cond.reshape(KC, 128, 2).transpose(1, 0, 2))
        posc = np.clip(pos, 0, 4095)
        rows = (posc // 64).astype(np.float64)
        cols = (posc % 64).astype(np.float64)
        ang_r = rows[:, None] * inv[None, :]
        ang_c = cols[:, None] * inv[None, :]
        ang = np.concatenate([ang_r, ang_r, ang_c, ang_c], axis=1)
        cosv = np.cos(ang).astype(f32).T
        sinv = (np.sin(ang) * sign[None, :].astype(np.float64)).astype(f32).T
        m["ropecos"] = np.ascontiguousarray(np.concatenate([cosv, cosv], axis=0))
        m["ropesin"] = np.ascontiguousarray(np.concatenate([sinv, sinv], axis=0))
        ck = cache_k[sbi, 0]
        ckT = ck.transpose(1, 2, 0)
        ckT = np.concatenate([ckT, ckT], axis=1)
        m["ckT"] = np.ascontiguousarray(ckT.transpose(1, 0, 2).reshape(128, 4 * 256))
        m["cv"] = np.ascontiguousarray(cache_v[sbi, 0].reshape(2, 128, 4, 64).transpose(1, 0, 2, 3))
        kval = np.zeros((128, 11), f32)
        okp = np.zeros(11 * 128, f32)
        okp[:SFN] = ok
        kval[:, :] = okp.reshape(11, 128).T
        m["kvalid"] = kval
        pinv = np.ones((4, 6, 8), f32)
        for gi, w in enumerate(POOL_W):
            pinv[gi] = 1.0 / w
            tl = np.arange(8)
            cl = (np.minimum(tl + w // 2, 256) - np.maximum(tl - w // 2, 0)).astype(f32)
            tr = 248 + np.arange(8)
            cr = (np.minimum(tr + w // 2, 256) - np.maximum(tr - w // 2, 0)).astype(f32)
            for e in (0, 2):
                pinv[gi, e] = 1.0 / cl
                pinv[gi, e + 1] = 1.0 / cr
            pl = s0 + np.arange(8)
            pr = s0 + 1024 - 8 + np.arange(8)
            pinv[gi, 4] = 1.0 / (np.minimum(pl + w // 2, 4096) - np.maximum(pl - w // 2, 0)).astype(f32)
            pinv[gi, 5] = 1.0 / (np.minimum(pr + w // 2, 4096) - np.maximum(pr - w // 2, 0)).astype(f32)
        m["pinv"] = np.ascontiguousarray(np.broadcast_to(pinv.reshape(1, -1), (128, 4 * 6 * 8)))
        pv = np.ones(16, f32)
        pv[:8] = ((s0 - 8 + np.arange(8)) >= 0)
        pv[8:] = ((s0 + 1024 + np.arange(8)) < 4096)
        m["pvalid"] = np.ascontiguousarray(np.broadcast_to(pv[None, :], (128, 16)))
        in_maps.append({k: np.ascontiguousarray(v, dtype=f32) for k, v in m.items()})
    return in_maps


_NC_CACHE = {}


def kernel(**inputs):
    inputs = {k: np.asarray(v) for k, v in inputs.items()}
    in_maps = prepare_inputs(**inputs)
    if "nc" not in _NC_CACHE:
        _NC_CACHE["nc"] = build_program()
    nc = _NC_CACHE["nc"]
    res = run_bass_kernel_spmd(nc, in_maps, core_ids=list(range(8)))
    y_prompt = np.empty((16, 256, D), np.float32)
    y_sample = np.empty((2, 4096, D), np.float32)
    state_k = np.empty((16, 1, 256, 4, 64), np.float32)
    state_v = np.empty((16, 1, 256, 4, 64), np.float32)
    for core in range(8):
        r = res.results[core]
        sbi, qd = core // 4, core % 4
        y = r["yT"].transpose(2, 1, 0).reshape(1536, D)
        y_prompt[2 * core] = y[0:256]
        y_prompt[2 * core + 1] = y[256:512]
        y_sample[sbi, qd * 1024:(qd + 1) * 1024] = y[512:]
        sk = r["skT"].transpose(2, 1, 0)
        state_k[2 * core, 0] = sk[0:256]
        state_k[2 * core + 1, 0] = sk[256:512]
        sv = r["sv"].transpose(1, 0, 2).reshape(512, 4, 64)
        state_v[2 * core, 0] = sv[0:256]
        state_v[2 * core + 1, 0] = sv[256:512]
    return (y_prompt, y_sample, state_k, state_v)
```
